# Optimizing a Trainium2 kernel written in Bass

```python
import math
import jax, jax.numpy as jnp
from jax import lax
import numpy as np

D_MODEL = 1024
BATCH = 8
SEQ = 2048
DEPTH = 1

N_HEADS_A = 8
HEAD_DIM_A = 64
V_DIM_A = 2 * HEAD_DIM_A
QK_WIDTH = N_HEADS_A * 2 * HEAD_DIM_A
ATTN_WIDTH = N_HEADS_A * V_DIM_A
Q_BLOCK = 128
N_GROUPS_B = 8
CHUNK = 128
GMLP_WIDTH = 1024
GROUP_DIM_B = GMLP_WIDTH // N_GROUPS_B
N_BUCKETS = 32
MAX_DISTANCE = 128
D_FF = 2816
N_SUBLAYERS = 3
N_BRANCHES = 2
EPS = 1e-6
COL_Q = 0
COL_K = COL_Q + QK_WIDTH
COL_V = COL_K + QK_WIDTH
COL_U = COL_V + ATTN_WIDTH
COL_GV = COL_U + GMLP_WIDTH
COL_GATE = COL_GV + GMLP_WIDTH
IN_COLS = COL_GATE + N_BRANCHES * D_MODEL

kernel_name = "hybrid_diffattn_gmlp_macaron_adaln"


def rmsnorm(x):
    xf = x.astype(jnp.float32)
    return (xf * lax.rsqrt(jnp.mean(xf * xf, axis=-1, keepdims=True) + EPS)).astype(x.dtype)


def layernorm(x, g, b):
    xf = x.astype(jnp.float32)
    mu = jnp.mean(xf, axis=-1, keepdims=True)
    var = jnp.mean(jnp.square(xf - mu), axis=-1, keepdims=True)
    return ((xf - mu) * lax.rsqrt(var + EPS)).astype(x.dtype) * g + b


def modulate(h, shift, scale):
    return h * (1.0 + scale[:, None, :]) + shift[:, None, :]


def swiglu(h, w_gate, w_up, w_down):
    return (jax.nn.silu(h @ w_gate) * (h @ w_up)) @ w_down


def rel_bucket(q_pos, k_pos):
    n = jnp.maximum(q_pos[:, None] - k_pos[None, :], 0)
    max_exact = N_BUCKETS // 2
    nf = jnp.maximum(n, 1).astype(jnp.float32)
    large = max_exact + (jnp.log(nf / max_exact) / math.log(MAX_DISTANCE / max_exact)
                         * (N_BUCKETS - max_exact)).astype(jnp.int32)
    large = jnp.minimum(large, N_BUCKETS - 1)
    return jnp.where(n < max_exact, n, large)


def diff_attention(q, k, v, rel_table, q_g, k_g, lam, lam_init, sub_g):
    b, s = q.shape[0], q.shape[1]
    q = rmsnorm(q) * q_g
    k = rmsnorm(k) * k_g
    scale = HEAD_DIM_A ** -0.5
    outs = []
    for i in range(s // Q_BLOCK):
        q0 = i * Q_BLOCK
        kv_len = q0 + Q_BLOCK
        qb = q[:, q0:kv_len]
        kb = k[:, :kv_len]
        vb = v[:, :kv_len]
        logits = jnp.einsum('bqhmd,bkhmd->bhmqk', qb, kb).astype(jnp.float32) * scale
        q_pos = q0 + jnp.arange(Q_BLOCK)
        k_pos = jnp.arange(kv_len)
        bias = jnp.transpose(rel_table[rel_bucket(q_pos, k_pos)], (2, 0, 1)).astype(jnp.float32)
        logits = logits + bias[None, :, None]
        mask = k_pos[None, :] <= q_pos[:, None]
        logits = jnp.where(mask, logits, -jnp.inf)
        p = jax.nn.softmax(logits, axis=-1)
        attn = p[:, :, 0] - lam * p[:, :, 1]
        outs.append(jnp.einsum('bhqk,bkhe->bqhe', attn.astype(vb.dtype), vb))
    o = jnp.concatenate(outs, axis=1)
    o = rmsnorm(o) * sub_g * (1.0 - lam_init)
    return o.reshape(b, s, ATTN_WIDTH)


def spatial_gating(u, v, ln_g, ln_b, w_s, b_s):
    b, s = u.shape[0], u.shape[1]
    v = layernorm(v, ln_g, ln_b)
    vc = v.reshape(b, s // CHUNK, CHUNK, N_GROUPS_B, GROUP_DIM_B)
    tri = jnp.tril(jnp.ones((CHUNK, CHUNK), dtype=bool))
    w = jnp.where(tri[None], w_s, jnp.zeros_like(w_s))
    f = jnp.einsum('gts,bnsgc->bntgc', w, vc) + jnp.transpose(b_s)[None, None, :, :, None]
    return u * f.reshape(b, s, GMLP_WIDTH)


def setup_inputs(seed: int = 0) -> dict:
    key = jax.random.key(seed)
    ks = jax.random.split(key, 32)
    f32 = jnp.float32
    L, D = DEPTH, D_MODEL

    def nrm(k, shape, fan_in):
        return jax.random.normal(k, shape, f32) * (fan_in ** -0.5)

    def gain(k, shape):
        return 1.0 + 0.05 * jax.random.normal(k, shape, f32)

    return {
        "x": jax.random.normal(ks[0], (BATCH, SEQ, D), f32),
        "c": jax.random.normal(ks[1], (BATCH, D), f32),
        "w_ada": nrm(ks[2], (L, D, N_SUBLAYERS * 3 * D), D) * 0.5,
        "b_ada": 0.02 * jax.random.normal(ks[3], (L, N_SUBLAYERS * 3 * D), f32),
        "w_ffn1_gate": nrm(ks[4], (L, D, D_FF), D),
        "w_ffn1_up": nrm(ks[5], (L, D, D_FF), D),
        "w_ffn1_down": nrm(ks[6], (L, D_FF, D), D_FF),
        "w_in": nrm(ks[7], (L, D, IN_COLS), D),
        "q_norm_g": gain(ks[8], (L, HEAD_DIM_A)),
        "k_norm_g": gain(ks[9], (L, HEAD_DIM_A)),
        "lam_q1": 0.1 * jax.random.normal(ks[10], (L, HEAD_DIM_A), f32),
        "lam_k1": 0.1 * jax.random.normal(ks[11], (L, HEAD_DIM_A), f32),
        "lam_q2": 0.1 * jax.random.normal(ks[12], (L, HEAD_DIM_A), f32),
        "lam_k2": 0.1 * jax.random.normal(ks[13], (L, HEAD_DIM_A), f32),
        "subln_g": gain(ks[14], (L, V_DIM_A)),
        "rel_bias_table": 0.5 * jax.random.normal(ks[15], (N_BUCKETS, N_HEADS_A), f32),
        "gmlp_ln_g": gain(ks[16], (L, GMLP_WIDTH)),
        "gmlp_ln_b": 0.02 * jax.random.normal(ks[17], (L, GMLP_WIDTH), f32),
        "w_spatial": nrm(ks[18], (L, N_GROUPS_B, CHUNK, CHUNK), CHUNK),
        "b_spatial": 1.0 + 0.05 * jax.random.normal(ks[19], (L, N_GROUPS_B, CHUNK), f32),
        "w_a_proj": nrm(ks[20], (L, ATTN_WIDTH, D), ATTN_WIDTH),
        "w_b_proj": nrm(ks[21], (L, GMLP_WIDTH, D), GMLP_WIDTH),
        "w_o": nrm(ks[22], (L, D, D), D),
        "w_ffn2_gate": nrm(ks[23], (L, D, D_FF), D),
        "w_ffn2_up": nrm(ks[24], (L, D, D_FF), D),
        "w_ffn2_down": nrm(ks[25], (L, D_FF, D), D_FF),
    }


def reference(x, c, w_ada, b_ada, w_ffn1_gate, w_ffn1_up, w_ffn1_down, w_in,
              q_norm_g, k_norm_g, lam_q1, lam_k1, lam_q2, lam_k2, subln_g,
              rel_bias_table, gmlp_ln_g, gmlp_ln_b, w_spatial, b_spatial,
              w_a_proj, w_b_proj, w_o, w_ffn2_gate, w_ffn2_up, w_ffn2_down):
    b, s, d = x.shape
    for l in range(DEPTH):
        lam_init = 0.8 - 0.6 * math.exp(-0.3 * l)
        mod = (jax.nn.silu(c) @ w_ada[l] + b_ada[l]).reshape(b, N_SUBLAYERS * 3, d)
        sh0, sc0, g0, sh1, sc1, g1, sh2, sc2, g2 = [mod[:, j] for j in range(N_SUBLAYERS * 3)]

        h = modulate(rmsnorm(x), sh0, sc0)
        x = x + 0.5 * g0[:, None, :] * swiglu(h, w_ffn1_gate[l], w_ffn1_up[l], w_ffn1_down[l])

        h = modulate(rmsnorm(x), sh1, sc1)
        proj = h @ w_in[l]
        q = proj[..., COL_Q:COL_K].reshape(b, s, N_HEADS_A, 2, HEAD_DIM_A)
        k = proj[..., COL_K:COL_V].reshape(b, s, N_HEADS_A, 2, HEAD_DIM_A)
        v = proj[..., COL_V:COL_U].reshape(b, s, N_HEADS_A, V_DIM_A)
        gu = jax.nn.gelu(proj[..., COL_U:COL_GV], approximate=False)
        gv = jax.nn.gelu(proj[..., COL_GV:COL_GATE], approximate=False)
        gates = jax.nn.sigmoid(proj[..., COL_GATE:].astype(jnp.float32)).astype(x.dtype)
        gates = gates.reshape(b, s, N_BRANCHES, d)

        lam = (jnp.exp(jnp.sum(lam_q1[l].astype(jnp.float32) * lam_k1[l].astype(jnp.float32)))
               - jnp.exp(jnp.sum(lam_q2[l].astype(jnp.float32) * lam_k2[l].astype(jnp.float32)))
               + lam_init)
        y_a = diff_attention(q, k, v, rel_bias_table, q_norm_g[l], k_norm_g[l], lam,
                             lam_init, subln_g[l]) @ w_a_proj[l]
        y_b = spatial_gating(gu, gv, gmlp_ln_g[l], gmlp_ln_b[l], w_spatial[l],
                             b_spatial[l]) @ w_b_proj[l]
        merged = gates[:, :, 0] * y_a + gates[:, :, 1] * y_b
        x = x + g1[:, None, :] * (merged @ w_o[l])

        h = modulate(rmsnorm(x), sh2, sc2)
        x = x + 0.5 * g2[:, None, :] * swiglu(h, w_ffn2_gate[l], w_ffn2_up[l], w_ffn2_down[l])
    return x
```

```python
import math
import numpy as np
from contextlib import ExitStack
import concourse.bass as bass
import concourse.mybir as mybir
from concourse.bass_utils import run_bass_kernel_spmd

F32 = mybir.dt.float32
BF16 = mybir.dt.bfloat16
U8 = mybir.dt.uint8
AF = mybir.ActivationFunctionType
ALU = mybir.AluOpType

D = 1024
S = 2048
DFF = 2816
NCH = 8
NT = 4
TB = 512
H = 8
EPS = 1e-6
COL_Q, COL_K, COL_V, COL_U, COL_GV, COL_GATE = 0, 1024, 2048, 3072, 4096, 5120
SLOT = 6144
NSLOT = 3
NFG = 11
LAM_INIT = 0.8 - 0.6 * math.exp(-0.3 * 0)
SEM_LIMIT = 30000

A_BADA = 0
A_GQ = 72
A_GK = 73
A_GSUB = 74
A_LAM = 75
NA = A_LAM + 256
B_TRI = 0
B_WST = 128
B_LNG = B_WST + 1024
B_LNB = B_LNG + 8
B_BS = B_LNB + 8
NB = B_BS + 1024


class Builder:
    ENGS = ("pe", "act", "dve", "pool", "sp")

    def __init__(self, nc, stack):
        self.nc = nc
        self.stack = stack
        self.ops = {e: [] for e in self.ENGS}
        self.sems = []
        self.cur = {}
        self.waited = {e: {} for e in self.ENGS}
        for e in self.ENGS:
            self.cur[e] = [self.new_sem("prog_" + e), 0]
        self.dma_cnt = {}

    def new_sem(self, name):
        h = self.stack.enter_context(self.nc.semaphore(name + "_%d" % len(self.sems)))
        self.sems.append(h)
        return len(self.sems) - 1

    def _waits(self, eng, deps):
        waits = []
        for t in deps:
            if t is None:
                continue
            s, v = t
            if self.waited[eng].get(s, 0) >= v:
                continue
            self.waited[eng][s] = v
            waits.append((s, v))
        return waits

    def op(self, eng, fn, deps=(), signal=True):
        waits = self._waits(eng, deps)
        inc = None
        tok = None
        if signal:
            cur = self.cur[eng]
            if cur[1] >= SEM_LIMIT:
                cur[0] = self.new_sem("prog_" + eng)
                cur[1] = 0
            cur[1] += 1
            inc = (cur[0], 1)
            tok = (cur[0], cur[1])
        self.ops[eng].append((waits, fn, inc))
        return tok

    def dma(self, eng, out, in_, sem, deps=()):
        waits = self._waits(eng, deps)
        self.dma_cnt[sem] = self.dma_cnt.get(sem, 0) + 16
        self.ops[eng].append((waits, lambda e: e.dma_start(out=out, in_=in_), (sem, 16)))
        return (sem, self.dma_cnt[sem])

    def wait_only(self, eng, deps):
        waits = self._waits(eng, deps)
        if waits:
            self.ops[eng].append((waits, None, None))

    def last(self, eng):
        c = self.cur[eng]
        return (c[0], c[1]) if c[1] > 0 else None

    def barrier(self):
        toks = {e: self.last(e) for e in ("pe", "act", "dve")}
        for e in ("pe", "act", "dve"):
            self.wait_only(e, [toks[o] for o in ("pe", "act", "dve") if o != e])
        return [t for t in toks.values() if t is not None]

    def emit(self):
        nc = self.nc
        with nc.Block() as block:
            def body(name):
                def f(e):
                    for waits, fn, inc in self.ops[name]:
                        for s, v in waits:
                            e.wait_ge(self.sems[s], v)
                        if fn is None:
                            continue
                        ins = fn(e)
                        if inc is not None:
                            ins.then_inc(self.sems[inc[0]], inc[1])
                return f
            block.tensor(body("pe"))
            block.scalar(body("act"))
            block.vector(body("dve"))
            block.gpsimd(body("pool"))
            block.sync(body("sp"))


class Rot:
    def __init__(self, aps):
        self.aps = aps
        self.i = -1
        self.readers = [[] for _ in aps]
        self.writer = [None for _ in aps]

    def next(self):
        self.i = (self.i + 1) % len(self.aps)
        deps = list(self.readers[self.i])
        if self.writer[self.i] is not None:
            deps.append(self.writer[self.i])
        self.readers[self.i] = []
        self.writer[self.i] = None
        return self.i, self.aps[self.i], deps

    def wrote(self, i, tok):
        self.writer[i] = tok

    def read(self, i, tok):
        self.readers[i].append(tok)


def _pk(W, c0, c1):
    n = c1 - c0
    return np.ascontiguousarray(W[:, c0:c1].reshape(8, 128, n).transpose(1, 0, 2).reshape(128, 8 * n))


def slice_plan():
    order = []
    for i in range(4):
        order.append(("ada", i))
    order.append(("ffn1", 0))
    order.append(("ada", 4))
    order.append(("ada", 5))
    nxt = 6
    for g in range(1, NFG):
        order.append(("ffn1", g))
        if nxt < 18:
            order.append(("ada", nxt))
            nxt += 1
    while nxt < 18:
        order.append(("ada", nxt))
        nxt += 1
    for h in range(H):
        order.append(("att", h))
    for i in range(4):
        order.append(("aproj", i))
    for i in range(2):
        order.append(("woa", i))
    for i in range(2):
        order.append(("u", i))
    for i in range(2):
        order.append(("gv", i))
    for i in range(4):
        order.append(("bproj", i))
    for i in range(2):
        order.append(("wob", i))
    for g in range(NFG):
        order.append(("ffn2", g))
    return order


def slice_size(kind):
    return {"ada": 4096, "ffn1": 6144, "ffn2": 6144, "att": 3072, "aproj": 4096, "woa": 4096,
            "u": 4096, "gv": 4096, "bproj": 4096, "wob": 4096}[kind]


def build_wblob(inp):
    w_ada = inp["w_ada"][0]
    w_in = inp["w_in"][0]
    parts = []
    for kind, i in slice_plan():
        if kind == "ada":
            a = _pk(w_ada, 512 * i, 512 * i + 512)
        elif kind in ("ffn1", "ffn2"):
            if kind == "ffn1":
                Wg, Wu, Wd = inp["w_ffn1_gate"][0], inp["w_ffn1_up"][0], inp["w_ffn1_down"][0]
            else:
                Wg, Wu, Wd = inp["w_ffn2_gate"][0], inp["w_ffn2_up"][0], inp["w_ffn2_down"][0]
            wd = Wd[256 * i:256 * i + 256, :].reshape(2, 128, 1024).transpose(1, 0, 2).reshape(128, 2048)
            a = np.concatenate([_pk(Wg, 256 * i, 256 * i + 256), _pk(Wu, 256 * i, 256 * i + 256), wd], axis=1)
        elif kind == "att":
            a = np.concatenate([_pk(w_in, COL_Q + 128 * i, COL_Q + 128 * i + 128),
                                _pk(w_in, COL_K + 128 * i, COL_K + 128 * i + 128),
                                _pk(w_in, COL_V + 128 * i, COL_V + 128 * i + 128)], axis=1)
        elif kind == "aproj":
            a = np.concatenate([_pk(inp["w_a_proj"][0], 256 * i, 256 * i + 256),
                                _pk(w_in, COL_GATE + 256 * i, COL_GATE + 256 * i + 256)], axis=1)
        elif kind == "bproj":
            a = np.concatenate([_pk(inp["w_b_proj"][0], 256 * i, 256 * i + 256),
                                _pk(w_in, COL_GATE + 1024 + 256 * i, COL_GATE + 1024 + 256 * i + 256)], axis=1)
        elif kind in ("woa", "wob"):
            a = _pk(inp["w_o"][0], 512 * i, 512 * i + 512)
        elif kind == "u":
            a = _pk(w_in, COL_U + 512 * i, COL_U + 512 * i + 512)
        elif kind == "gv":
            a = _pk(w_in, COL_GV + 512 * i, COL_GV + 512 * i + 512)
        assert a.shape == (128, slice_size(kind)), (kind, a.shape)
        parts.append(a.astype(np.float32, copy=False))
    return np.ascontiguousarray(np.concatenate(parts, axis=1))


def rel_bucket_np(n):
    n = np.asarray(n)
    max_exact = 16
    nf = np.maximum(n, 1).astype(np.float32)
    large = max_exact + (np.log(nf / np.float32(max_exact)) / np.float32(math.log(128 / max_exact))
                         * np.float32(32 - max_exact)).astype(np.int32)
    large = np.minimum(large, 31)
    return np.where(n < max_exact, n, large)


def build_consts(inp):
    cA = np.zeros((128, NA), np.float32)
    cA[:, A_BADA:A_BADA + 72] = inp["b_ada"][0].reshape(72, 128).T
    cA[:, A_GQ] = np.tile(inp["q_norm_g"][0], 2)
    cA[:, A_GK] = np.tile(inp["k_norm_g"][0], 2)
    cA[:, A_GSUB] = inp["subln_g"][0]
    lam = np.concatenate([inp["lam_q1"][0], inp["lam_k1"][0], inp["lam_q2"][0], inp["lam_k2"][0]])
    cA[:, A_LAM:A_LAM + 256] = lam[None, :]
    cB = np.zeros((128, NB), np.float32)
    s = np.arange(128)
    cB[:, B_TRI:B_TRI + 128] = (s[:, None] <= s[None, :]).astype(np.float32)
    cB[:, B_WST:B_WST + 1024] = inp["w_spatial"][0].transpose(2, 0, 1).reshape(128, 1024)
    cB[:, B_LNG:B_LNG + 8] = inp["gmlp_ln_g"][0].reshape(8, 128).T
    cB[:, B_LNB:B_LNB + 8] = inp["gmlp_ln_b"][0].reshape(8, 128).T
    cB[:, B_BS:B_BS + 1024] = inp["b_spatial"][0].reshape(1, 1024)
    cC = np.zeros((32, 8 + 256), np.float32)
    cC[:, 0:8] = inp["rel_bias_table"]
    bk = rel_bucket_np(np.arange(256))
    oh = np.zeros((32, 256), np.float32)
    oh[bk, np.arange(256)] = 1.0
    oh[31, :] -= 1.0
    cC[:, 8:] = oh
    return cA, cB, cC


def build_program(stage="full"):
    nc = bass.Bass("TRN2", target_bir_lowering=False)
    plan = slice_plan()
    offs = []
    o = 0
    for kind, i in plan:
        offs.append(o)
        o += slice_size(kind)
    WTOT = o
    sidx = {k: j for j, k in enumerate(plan)}

    xT_d = nc.dram_tensor("xT", [D, S], F32, kind="ExternalInput").ap()
    cv_d = nc.dram_tensor("cv", [128, 8], F32, kind="ExternalInput").ap()
    cA_d = nc.dram_tensor("cA", [128, NA], F32, kind="ExternalInput").ap()
    cB_d = nc.dram_tensor("cB", [128, NB], F32, kind="ExternalInput").ap()
    cC_d = nc.dram_tensor("cC", [32, 264], F32, kind="ExternalInput").ap()
    wb_d = nc.dram_tensor("wblob", [128, WTOT], F32, kind="ExternalInput").ap()
    out_d = nc.dram_tensor("outT", [D, S], F32, kind="ExternalOutput").ap()
    scr_h = nc.dram_tensor("ebscr", [8, 384], BF16, kind="Internal")
    xT_v = xT_d.rearrange("(c p) t -> p c t", p=128)
    out_v = out_d.rearrange("(c p) t -> p c t", p=128)

    with ExitStack() as st:
        B = Builder(nc, st)
        ARENA = 212000
        arena = st.enter_context(nc.sbuf_tensor("arena", [128, ARENA], U8))
        psum = st.enter_context(nc.psum_tensor("psum", [128, 8, 512], F32))

        def view(off, dtype, n):
            nb = n * (4 if dtype == F32 else 2)
            assert off % 4 == 0 and off + nb <= ARENA, (off, nb)
            return arena[:, off:off + nb].bitcast(dtype)

        X_OFF, H_OFF, W_OFF, C_OFF, P_OFF = 0, 65536, 98304, 98304 + NSLOT * SLOT * 2, 98304 + NSLOT * SLOT * 2 + 8192
        xT = view(X_OFF, F32, 8 * S).rearrange("p (c t) -> p c t", c=8)
        hT = view(H_OFF, BF16, 8 * S).rearrange("p (c t) -> p c t", c=8)
        slots = [view(W_OFF + i * SLOT * 2, BF16, SLOT) for i in range(NSLOT)]
        co = [C_OFF]

        def calloc(dtype, n):
            v = view(co[0], dtype, n)
            co[0] += ((n * (4 if dtype == F32 else 2) + 3) // 4) * 4
            assert co[0] <= P_OFF
            return v
        cA = calloc(F32, NA)
        modT = calloc(F32, 72)
        der = calloc(F32, 48)
        cvs = calloc(F32, 8)
        sc_bf = calloc(BF16, 8)
        ones_f = calloc(F32, 128)
        bo_f = calloc(F32, 128)
        ones_b = calloc(BF16, 128)
        EB2 = calloc(BF16, 2048).rearrange("p (h q) -> p h q", h=8)
        EBd = EB2[:, :, 0:128]
        EBs = EB2[:, :, 128:256]
        lamw = calloc(F32, 16)
        so = ARENA - 4096
        cC = view(so, F32, 264)
        ebrow = view(so + 1056, BF16, 384)
        lamtmp = view(so + 1056 + 768, F32, 128)

        po = [P_OFF]

        def preset():
            po[0] = P_OFF

        def palloc(dtype, n):
            v = view(po[0], dtype, n)
            po[0] += ((n * (4 if dtype == F32 else 2) + 3) // 4) * 4
            return v

        wsem = [B.new_sem("wslot") for _ in range(NSLOT)]
        W = {"next": 0, "load": {}, "rel": {}}

        def w_ensure(i):
            while W["next"] <= i and W["next"] < len(plan):
                j = W["next"]
                assert j < NSLOT or (j - NSLOT) in W["rel"], (j, plan[j])
                deps = W["rel"].get(j - NSLOT, []) if j >= NSLOT else []
                if j == 0:
                    deps = list(deps) + [x_tok[0], x_tok[1], x_tok[2]]
                n = slice_size(plan[j][0])
                W["load"][j] = B.dma("pool", slots[j % NSLOT][:, 0:n], wb_d[:, offs[j]:offs[j] + n],
                                     wsem[j % NSLOT], deps)
                W["next"] += 1

        def w_get(key):
            i = sidx[key]
            w_ensure(i)
            for j in range(i + 1, i + NSLOT):
                if j - NSLOT < 0 or (j - NSLOT) in W["rel"]:
                    w_ensure(j)
                else:
                    break
            return slots[i % NSLOT], W["load"][i]

        def w_release(key, toks):
            W["rel"][sidx[key]] = list(toks)

        bank_rd = [[] for _ in range(8)]

        def mm_group(out_ap, pairs, deps, bank, start=True, stop=True, sig=True):
            tok = None
            n = len(pairs)
            for i, (l, r) in enumerate(pairs):
                d = list(deps) + bank_rd[bank] if i == 0 else ()
                if i == 0 and start:
                    bank_rd[bank] = []
                tok = B.op("pe", lambda e, l=l, r=r, s0=(start and i == 0), s1=(stop and i == n - 1):
                           e.matmul(out_ap, l, r, start=s0, stop=s1),
                           deps=d, signal=(sig and i == n - 1))
            return tok

        s_c = B.new_sem("cload")
        t_cA = B.dma("sp", cA[:], cA_d, s_c)
        t_cv = B.dma("sp", cvs[:], cv_d, s_c)
        t_cC = B.dma("sp", cC[0:32, :], cC_d, s_c)
        t_cl = (s_c, 48)
        xsem = [B.new_sem("xload") for _ in range(NT)]
        x_tok = [B.dma("sp", xT[:, :, T * TB:(T + 1) * TB], xT_v[:, :, T * TB:(T + 1) * TB], xsem[T]) for T in range(NT)]

        t = B.op("dve", lambda e: e.memset(ones_f[:], 1.0))
        t = B.op("dve", lambda e: e.memset(bo_f[:], 0.0))
        t = B.op("dve", lambda e: e.memset(bo_f[0:64, 0:64], 1.0), deps=[t])
        t = B.op("dve", lambda e: e.memset(bo_f[64:128, 64:128], 1.0), deps=[t])
        t = B.op("dve", lambda e: e.memset(ones_b[:], 1.0))
        t_ebz = B.op("dve", lambda e: e.memset(ebrow[0:8, :], 0.0))
        t_const = t_ebz
        t_sc = B.op("act", lambda e: e.activation(sc_bf[:], cvs[:], AF.Silu), deps=[t_cl])

        lv = cA[:, A_LAM:A_LAM + 256]
        t1 = B.op("dve", lambda e: e.tensor_tensor(lamtmp[:, 0:64], lv[:, 0:64], lv[:, 64:128], ALU.mult), deps=[t_cl])
        t2 = B.op("dve", lambda e: e.tensor_tensor(lamtmp[:, 64:128], lv[:, 128:192], lv[:, 192:256], ALU.mult), deps=[t_cl])
        t1 = B.op("dve", lambda e: e.reduce_sum(lamw[:, 0:1], lamtmp[:, 0:64], mybir.AxisListType.X), deps=[t1])
        t2 = B.op("dve", lambda e: e.reduce_sum(lamw[:, 1:2], lamtmp[:, 64:128], mybir.AxisListType.X), deps=[t2])
        t3 = B.op("act", lambda e: e.activation(lamw[:, 2:4], lamw[:, 0:2], AF.Exp), deps=[t1, t2])
        t4 = B.op("dve", lambda e: e.tensor_tensor(lamw[:, 4:5], lamw[:, 3:4], lamw[:, 2:3], ALU.subtract), deps=[t3])
        t4 = B.op("dve", lambda e: e.tensor_scalar(lamw[:, 4:5], lamw[:, 4:5], -LAM_INIT, None, ALU.add), deps=[t4])
        t5 = B.op("dve", lambda e: e.tensor_scalar(lamw[:, 5:6], cA[:, A_GSUB:A_GSUB + 1], 1.0 - LAM_INIT, None, ALU.mult), deps=[t_cl])
        t_lam = t5
        neg_lam = lamw[:, 4:5]
        gsub8 = lamw[:, 5:6]

        tb = mm_group(psum[0:8, 7, 0:256], [(cC[0:32, 0:8], cC[0:32, 8:264])], [t_cl], 7)
        te = B.op("act", lambda e: e.activation(ebrow[0:8, 127:383], psum[0:8, 7, 0:256], AF.Exp), deps=[tb, t_ebz])
        bank_rd[7].append(te)
        s_eb = B.new_sem("eb")
        tw = B.dma("sp", scr_h.ap(), ebrow[0:8, :], s_eb, deps=[te])
        for k in range(128):
            B.dma("sp", EB2[k:k + 1, :, :], bass.AP(scr_h, 127 - k, [[0, 1], [384, 8], [1, 256]]), s_eb, deps=[tw])
        t_eb = (s_eb, B.dma_cnt[s_eb])

        modps = psum[:, 7, 256:328]
        mod_tok = {}

        def ada_slice(i):
            slot, lt = w_get(("ada", i))
            w = slot[:, 0:4096].rearrange("p (k n) -> p k n", k=8)
            tok = None
            for sub in range(4):
                jc = 4 * i + sub
                tok = mm_group(modps[:, jc:jc + 1],
                               [(w[:, kc, sub * 128:(sub + 1) * 128], sc_bf[:, kc:kc + 1]) for kc in range(8)],
                               [lt, t_sc], 7)
            w_release(("ada", i), [tok])
            return tok

        hg_dep = []

        def mod_finish0(part, tok):
            a, b = (0, 16) if part == 0 else (16, 24)
            t = B.op("dve", lambda e: e.tensor_tensor(modT[:, a:b], modps[:, a:b], cA[:, A_BADA + a:A_BADA + b], ALU.add),
                     deps=[tok, t_cl])
            if part == 0:
                t2 = B.op("dve", lambda e: e.tensor_scalar(der[:, 0:8], modT[:, 8:16], 1.0, None, ALU.add), deps=[t])
                mod_tok[0] = t2
            else:
                t2 = B.op("dve", lambda e: e.tensor_scalar(der[:, 8:16], modT[:, 16:24], 0.5, None, ALU.mult), deps=[t])
                hg_dep.append(t2)
            return t2

        def mod_finish(k, tok):
            a, b = 24 * k, 24 * k + 24
            t = B.op("dve", lambda e: e.tensor_tensor(modT[:, a:b], modps[:, a:b], cA[:, A_BADA + a:A_BADA + b], ALU.add),
                     deps=[tok, t_cl])
            dcol = {0: 0, 1: 16, 2: 24}[k]
            t2 = B.op("dve", lambda e: e.tensor_scalar(der[:, dcol:dcol + 8], modT[:, a + 8:a + 16], 1.0, None, ALU.add), deps=[t])
            if k != 1:
                hcol = 8 if k == 0 else 32
                t2 = B.op("dve", lambda e: e.tensor_scalar(der[:, hcol:hcol + 8], modT[:, a + 16:a + 24], 0.5, None, ALU.mult), deps=[t])
            mod_tok[k] = t2
            return t2

        def make_norm(sh, sc1p, statbank, off, nrs=2, nsq=2):
            save = po[0]
            po[0] = off
            sq = Rot([palloc(BF16, TB) for _ in range(nsq)])
            lb = palloc(F32, TB)
            tmp = Rot([palloc(F32, TB) for _ in range(2)])
            rs = Rot([palloc(F32, TB) for _ in range(nrs)])
            po[0] = save
            stt = {"lb_rd": []}

            def block(T, xdeps, moddeps, hwar, phase="both", info=None):
                tsl = slice(T * TB, (T + 1) * TB)
                if phase == "apply":
                    i, rap, tr = info
                    return apply_(T, tsl, i, rap, tr, xdeps, moddeps, hwar)
                tok = None
                if phase == "sq":
                    out = []
                    for c in range(NCH):
                        i, ap, deps = sq.next()
                        ta = B.op("act", lambda e, ap=ap, c=c, tsl=tsl: e.activation(ap[:], xT[:, c, tsl], AF.Square),
                                  deps=deps + xdeps)
                        sq.wrote(i, ta)
                        out.append((i, ap, ta))
                    return out
                for c in range(NCH):
                    if phase == "st":
                        i, ap, ta = info[c]
                    else:
                        i, ap, deps = sq.next()
                        ta = B.op("act", lambda e, ap=ap, c=c, tsl=tsl: e.activation(ap[:], xT[:, c, tsl], AF.Square),
                                  deps=deps + xdeps)
                        sq.wrote(i, ta)
                    tok = mm_group(psum[:, statbank, :], [(ones_b[:], ap[:])], [ta, t_const], statbank,
                                   start=(c == 0), stop=(c == NCH - 1))
                    sq.read(i, tok)
                tl = B.op("act", lambda e: e.activation(lb[:], psum[:, statbank, :], AF.Ln, bias=EPS, scale=1.0 / D),
                          deps=[tok] + stt["lb_rd"] + xdeps)
                bank_rd[statbank].append(tl)
                i, rap, deps = rs.next()
                tr = B.op("act", lambda e, rap=rap: e.activation(rap[:], lb[:], AF.Exp, scale=-0.5), deps=deps + [tl])
                stt["lb_rd"] = [tr]
                rs.wrote(i, tr)
                if phase in ("stats", "st"):
                    return (i, rap, tr)
                return apply_(T, tsl, i, rap, tr, xdeps, moddeps, hwar)

            def apply_(T, tsl, i, rap, tr, xdeps, moddeps, hwar):
                th = None
                for c in range(NCH):
                    j, tap, deps = tmp.next()
                    td = B.op("dve", lambda e, tap=tap, rap=rap, c=c, tsl=tsl: e.scalar_tensor_tensor(
                        tap[:], xT[:, c, tsl], sc1p[:, c:c + 1], rap[:], ALU.mult, ALU.mult),
                        deps=deps + [tr] + xdeps + moddeps)
                    tmp.wrote(j, td)
                    rs.read(i, td)
                    th = B.op("act", lambda e, tap=tap, c=c, tsl=tsl: e.activation(hT[:, c, tsl], tap[:], AF.Identity, bias=sh[:, c:c + 1]),
                              deps=[td] + hwar + moddeps)
                    tmp.read(j, th)
                return th
            return block

        def ffn_phase(name, h_tok, hg, ada_iter, final_out, epilogue=None, early=None):
            gu_pe = {}
            sgb = Rot([palloc(F32, TB) for _ in range(2)])
            hid = Rot([palloc(BF16, 2 * TB).rearrange("p (j t) -> p j t", j=2) for _ in range(2)])
            gbanks, ubanks, dbanks = Rot([0, 1]), Rot([2, 3]), Rot([4, 5, 6])
            x_last = {}
            out_toks = []
            wts = {}

            def get_w(g):
                if g not in wts:
                    slot, lt = w_get((name, g))
                    wts[g] = (slot[:, 0:2048].rearrange("p (k n) -> p k n", k=8),
                              slot[:, 2048:4096].rearrange("p (k n) -> p k n", k=8),
                              slot[:, 4096:6144].rearrange("p (j n) -> p j n", j=2), lt)
                return wts[g]

            def emit_gu(g, T):
                wg, wu, wd, lt = get_w(g)
                tsl = slice(T * TB, (T + 1) * TB)
                hi, hap, hdeps = hid.next()
                hw = []
                for j in range(2):
                    _, gb, _ = gbanks.next()
                    _, ub, _ = ubanks.next()
                    tg = mm_group(psum[:, gb, :], [(wg[:, kc, j * 128:(j + 1) * 128], hT[:, kc, tsl]) for kc in range(8)],
                                  [lt, h_tok[T]], gb)
                    tu = mm_group(psum[:, ub, :], [(wu[:, kc, j * 128:(j + 1) * 128], hT[:, kc, tsl]) for kc in range(8)],
                                  [lt, h_tok[T]], ub)
                    si, sap, sdeps = sgb.next()
                    ta = B.op("act", lambda e, sap=sap, gb=gb: e.activation(sap[:], psum[:, gb, :], AF.Silu), deps=sdeps + [tg])
                    bank_rd[gb].append(ta)
                    sgb.wrote(si, ta)
                    td = B.op("dve", lambda e, hap=hap, j=j, sap=sap, ub=ub: e.tensor_tensor(hap[:, j, :], sap[:], psum[:, ub, :], ALU.mult),
                              deps=hdeps + [ta, tu])
                    bank_rd[ub].append(td)
                    sgb.read(si, td)
                    hw.append(td)
                    gu_pe[(g, T)] = tu
                hid.wrote(hi, hw[-1])
                return (hi, hap, hw)

            def emit_down(g, T, hinfo):
                wg, wu, wd, lt = get_w(g)
                hi, hap, hw = hinfo
                tsl = slice(T * TB, (T + 1) * TB)
                last_pe = None
                for d in range(NCH):
                    _, db, _ = dbanks.next()
                    tdn = mm_group(psum[:, db, :], [(wd[:, j, d * 128:(d + 1) * 128], hap[:, j, :]) for j in range(2)], hw, db)
                    hid.read(hi, tdn)
                    last_pe = tdn
                    prev = x_last.get((d, T))
                    tx = B.op("dve", lambda e, d=d, db=db, tsl=tsl: e.scalar_tensor_tensor(
                        xT[:, d, tsl], psum[:, db, :], hg[:, d:d + 1], xT[:, d, tsl], ALU.mult, ALU.add),
                        deps=[tdn, prev] + hg_dep)
                    bank_rd[db].append(tx)
                    x_last[(d, T)] = tx
                    if final_out and g == NFG - 1:
                        out_toks.append(B.dma("sp", out_v[:, d, tsl], xT[:, d, tsl], s_out, deps=[tx]))
                return last_pe

            units = [(g, T) for g in range(NFG) for T in range(NT)]
            nxt = emit_gu(*units[0])
            for i, (g, T) in enumerate(units):
                cur = nxt
                defer = early is not None and i + 1 < len(units) and units[i + 1] == (1, 0)
                if i + 1 < len(units) and not defer:
                    nxt = emit_gu(*units[i + 1])
                if i == 0 and early is not None:
                    early()
                last_pe = emit_down(g, T, cur)
                if epilogue is not None and g == NFG - 1:
                    epilogue(T, [x_last[(NCH - 1, T)]], [gu_pe[(g, T)]])
                if T == NT - 1:
                    w_release((name, g), [last_pe])
                    if ada_iter is not None:
                        ada_iter(g)
                if defer:
                    nxt = emit_gu(*units[i + 1])
            xd = [x_last[(NCH - 1, T)] for T in range(NT)]
            return xd, out_toks

        s_out = B.new_sem("out")

        U_OFF = P_OFF + 8 * S * 2
        preset()
        nb0 = make_norm(modT[:, 0:8], der[:, 0:8], 6, P_OFF, nrs=4)
        st0 = [nb0(T, [x_tok[T], t_const], [], [], phase="stats") for T in range(NT)]
        tk = None
        for i in range(4):
            tk = ada_slice(i)
        mod_finish0(0, tk)
        ada_next = [6]

        def early0():
            tk = ada_slice(4)
            tk = ada_slice(5)
            mod_finish0(1, tk)

        def ada_iter(g):
            if g == 0:
                return
            if ada_next[0] < 18:
                i = ada_next[0]
                ada_next[0] += 1
                tk = ada_slice(i)
                if i == 11:
                    mod_finish(1, tk)
                if i == 17:
                    mod_finish(2, tk)
            if g == NFG - 1:
                while ada_next[0] < 18:
                    ada_iter(-1)

        po[0] = P_OFF + 14336 + 4096
        h_tok0 = [nb0(T, [x_tok[T], t_const], [mod_tok[0]], [], phase="apply", info=st0[T]) for T in range(NT)]
        nb1 = make_norm(modT[:, 24:32], der[:, 16:24], 6, U_OFF, nsq=8)
        h_tok1 = [None] * NT
        pend1 = []

        def flush1():
            while pend1:
                T, sqs, xtoks, hrd = pend1.pop(0)
                info = nb1(T, xtoks, [], [], phase="st", info=sqs)
                h_tok1[T] = nb1(T, xtoks, [mod_tok[1]], hrd, phase="apply", info=info)

        def epi1(T, xtoks, hrd):
            flush1()
            pend1.append((T, nb1(T, xtoks, [], [], phase="sq"), xtoks, hrd))

        xd, _ = ffn_phase("ffn1", h_tok0, der[:, 8:16], ada_iter, False, epilogue=(None if stage == "ffn1" else epi1), early=early0)
        flush1()

        def finish(xdeps):
            toks = []
            for T in range(NT):
                for d in range(NCH):
                    toks.append(B.dma("sp", out_v[:, d, T * TB:(T + 1) * TB], xT[:, d, T * TB:(T + 1) * TB], s_out, deps=xdeps))
            B.wait_only("sp", [(s_out, B.dma_cnt[s_out])])

        if stage == "ffn1":
            bt = B.barrier()
            finish(bt)
            B.emit()
            return nc

        class _Stop(Exception):
            pass

        def stop_if(name):
            if stage == name:
                raise _Stop()

        bt = B.barrier()
        for e_ in ("pe", "act", "dve"):
            B.wait_only(e_, [t_eb, t_cl])
        preset()
        dbg_dump = []
        g1 = modT[:, 40:48]
        bufA = palloc(BF16, 8 * S).rearrange("p (c t) -> p c t", c=8)
        assert po[0] == U_OFF
        h_tok = [None] * NT
        if stage == 'mnorm':
            finish(bt); B.emit(); return nc
        po[0] = U_OFF
        qz = palloc(BF16, 2 * S).rearrange("p (m t) -> p m t", m=2)
        kT = palloc(BF16, S)
        vh = palloc(BF16, S).rearrange("p (t e) -> p t e", e=128)
        Eb = Rot([palloc(BF16, 512).rearrange("p (m q) -> p m q", m=2) for _ in range(4)])
        rSb = Rot([palloc(F32, 512).rearrange("p (m q) -> p m q", m=2) for _ in range(1)])
        odb = Rot([palloc(F32, 256) for _ in range(2)])
        sqb = Rot([palloc(BF16, 256) for _ in range(2)])
        rdb = Rot([palloc(F32, 256) for _ in range(2)])
        psqb = Rot([palloc(BF16, TB) for _ in range(2)])
        plrb = Rot([palloc(F32, TB) for _ in range(2)])
        bo_b = palloc(BF16, 128)
        assert po[0] <= ARENA, po[0]

        Sb, Ob, Sumb, Pb, PbC = Rot([0, 1, 6]), Rot([2, 3]), Rot([4, 5]), Rot([7, 4, 5, 2, 3]), Rot([7])
        t_qz = B.op("dve", lambda e: e.memset(qz[:], 0.0), deps=bt)
        t_bob = B.op("dve", lambda e: e.tensor_copy(bo_b[:], bo_f[:]), deps=bt)
        q_rd, k_rd, v_rd = list(bt) + [t_qz], list(bt), list(bt)
        gq = cA[:, A_GQ:A_GQ + 1]
        gk = cA[:, A_GK:A_GK + 1]

        for h in range(H):
            slot, lt = w_get(("att", h))
            wq = slot[:, 0:1024].rearrange("p (k n) -> p k n", k=8)
            wk = slot[:, 1024:2048].rearrange("p (k n) -> p k n", k=8)
            wv = slot[:, 2048:3072].rearrange("p (k n) -> p k n", k=8)
            units = [("q", T) for T in range(NT)] + [("k", T) for T in range(NT)]
            q_tok, k_tok = [None] * NT, [None] * NT
            pend = None

            def proj_mm(u):
                kind, T = u
                tsl = slice(T * TB, (T + 1) * TB)
                w = wq if kind == "q" else wk
                _, pb, _ = Pb.next()
                tp = mm_group(psum[:, pb, :], [(w[:, kc, :], hT[:, kc, tsl]) for kc in range(8)], [lt], pb)
                si, sq_ap, sdeps = psqb.next()
                ts = B.op("act", lambda e, pb=pb, sq_ap=sq_ap: e.activation(sq_ap[:], psum[:, pb, :], AF.Square), deps=[tp] + sdeps)
                psqb.wrote(si, ts)
                bank_rd[pb].append(ts)
                return (kind, T, tsl, pb, tp, si, sq_ap, ts)

            def proj_fin(st_):
                kind, T, tsl, pb, tp, si, sq_ap, ts = st_
                _, sbk, _ = Sb.next()
                tst = mm_group(psum[:, sbk, :], [(bo_b[:], sq_ap[:])], [ts, t_bob], sbk)
                psqb.read(si, tst)
                li, lr, ldeps = plrb.next()
                tl = B.op("act", lambda e, sbk=sbk, lr=lr: e.activation(lr[:], psum[:, sbk, :], AF.Ln, bias=EPS, scale=1.0 / 64),
                          deps=[tst] + ldeps)
                bank_rd[sbk].append(tl)
                tr = B.op("act", lambda e, lr=lr: e.activation(lr[:], lr[:], AF.Exp, scale=-0.5), deps=[tl])
                if kind == "q":
                    B.op("dve", lambda e, pb=pb, tsl=tsl, lr=lr: e.scalar_tensor_tensor(
                        qz[0:64, 0, tsl], psum[0:64, pb, :], gq[0:64, :], lr[0:64, :], ALU.mult, ALU.mult),
                        deps=[tr, tp, t_cl] + q_rd)
                    tq = B.op("dve", lambda e, pb=pb, tsl=tsl, lr=lr: e.scalar_tensor_tensor(
                        qz[64:128, 1, tsl], psum[64:128, pb, :], gq[64:128, :], lr[64:128, :], ALU.mult, ALU.mult),
                        deps=[tr, tp, t_cl] + q_rd)
                    q_tok[T] = tq
                else:
                    tq = B.op("dve", lambda e, pb=pb, tsl=tsl, lr=lr: e.scalar_tensor_tensor(
                        kT[:, tsl], psum[:, pb, :], gk, lr[:], ALU.mult, ALU.mult), deps=[tr, tp, t_cl] + k_rd)
                    k_tok[T] = tq
                bank_rd[pb].append(tq)
                plrb.wrote(li, tq)

            for u in units:
                st_ = proj_mm(u)
                if pend is not None:
                    proj_fin(pend)
                pend = st_
            v_tok = []
            last_v = None
            for tg in range(4):
                _, pb, _ = Pb.next()
                tv = None
                for tt in range(4):
                    t0 = (tg * 4 + tt) * 128
                    tv = mm_group(psum[:, pb, tt * 128:(tt + 1) * 128],
                                  [(hT[:, kc, t0:t0 + 128], wv[:, kc, :]) for kc in range(8)],
                                  [lt], pb, start=True)
                if pend is not None:
                    proj_fin(pend)
                    pend = None
                tc = B.op("dve", lambda e, pb=pb, tg=tg: e.tensor_copy(
                    vh[:, tg * 4:(tg + 1) * 4, :], psum[:, pb, :].rearrange("p (t e) -> p t e", e=128)),
                    deps=[tv] + v_rd)
                bank_rd[pb].append(tc)
                v_tok.append(tc)
                last_v = tv
            w_release(("att", h), [last_v])
            q_rd, k_rd, v_rd = [], [], []
            if stage == 'proj0':
                bt = B.barrier(); finish(bt); B.emit(); return nc

            pairs = [(Qb, j) for Qb in range(8) for j in range(2 * Qb + 2)]
            blk = {}

            def emit_S(pair):
                Qb, j = pair
                q0b = Qb * 256
                r = j - 2 * Qb
                qs = 128 if r == 1 else 0
                _, sbk, _ = Sb.next()
                Sv = psum[:, sbk, :].rearrange("p (m q) -> p m q", m=2)
                if qs == 0:
                    ts = B.op("pe", lambda e, Sv=Sv, j=j, q0b=q0b: e.matmul(
                        Sv[:, :, :], kT[:, j * 128:(j + 1) * 128], qz[:, :, q0b:q0b + 256], start=True, stop=True),
                        deps=[q_tok[q0b // TB], k_tok[(j * 128) // TB]] + bank_rd[sbk], signal=True)
                else:
                    for m in range(2):
                        ts = B.op("pe", lambda e, Sv=Sv, j=j, q0b=q0b, m=m: e.matmul(
                            Sv[:, m, 128:256], kT[:, j * 128:(j + 1) * 128], qz[:, m, q0b + 128:q0b + 256], start=True, stop=True),
                            deps=([q_tok[q0b // TB], k_tok[(j * 128) // TB]] + bank_rd[sbk]) if m == 0 else [], signal=(m == 1))
                bank_rd[sbk] = []
                ei, E, edeps = Eb.next()
                te = B.op("act", lambda e, E=E, Sv=Sv, qs=qs: e.activation(E[:, :, qs:256], Sv[:, :, qs:256], AF.Exp, scale=0.125),
                          deps=edeps + [ts])
                bank_rd[sbk].append(te)
                fix = []
                if r == -1:
                    fix = [(0, EBs)]
                elif r == 0:
                    fix = [(0, EBd), (128, EBs)]
                elif r == 1:
                    fix = [(128, EBd)]
                tl = te
                for (c0, EBt) in fix:
                    a = EBt[:, h, :]
                    ebb = bass.AP(a.tensor, a.offset, [list(a.ap[0]), [0, 2], list(a.ap[1])])
                    tl = B.op("dve", lambda e, E=E, c0=c0, ebb=ebb: e.tensor_tensor(
                        E[:, :, c0:c0 + 128], E[:, :, c0:c0 + 128], ebb, ALU.mult), deps=[te, t_eb])
                Eb.wrote(ei, tl)
                return (ei, E, qs, tl, te)

            def emit_PV(pair, einfo):
                Qb, j = pair
                ei, E, qs, tl, te = einfo
                nj = 2 * Qb + 2
                if j == 0:
                    _, ob, _ = Ob.next()
                    _, smb, _ = Sumb.next()
                    blk[Qb] = (ob, smb)
                ob, smb = blk[Qb]
                Ov = psum[:, ob, :].rearrange("p (m q) -> p m q", m=2)
                Smv = psum[:, smb, :].rearrange("p (m q) -> p m q", m=2)
                first, lastj = (j == 0), (j == nj - 1)
                if qs == 0:
                    B.op("pe", lambda e, Ov=Ov, E=E, j=j, first=first, lastj=lastj: e.matmul(
                        Ov[:, :, :], vh[:, j, :], E[:, :, :], start=first, stop=lastj),
                        deps=[tl, te, v_tok[j // 4]] + (bank_rd[ob] if first else []), signal=False)
                else:
                    for m in range(2):
                        B.op("pe", lambda e, Ov=Ov, E=E, j=j, m=m, first=first, lastj=lastj: e.matmul(
                            Ov[:, m, 128:256], vh[:, j, :], E[:, m, 128:256], start=first, stop=(lastj and m == 1)),
                            deps=([tl, te, v_tok[j // 4]] + (bank_rd[ob] if first else [])) if m == 0 else [], signal=False)
                if first:
                    bank_rd[ob] = []
                if qs == 0:
                    tpv = B.op("pe", lambda e, Smv=Smv, E=E, first=first, lastj=lastj: e.matmul(
                        Smv[:, :, :], ones_b[:], E[:, :, :], start=first, stop=lastj),
                        deps=(bank_rd[smb] if first else []), signal=True)
                else:
                    for m in range(2):
                        tpv = B.op("pe", lambda e, Smv=Smv, E=E, m=m, first=first, lastj=lastj: e.matmul(
                            Smv[:, m, 128:256], ones_b[:], E[:, m, 128:256], start=first, stop=(lastj and m == 1)),
                            deps=(bank_rd[smb] if (first and m == 0) else []), signal=(m == 1))
                if first:
                    bank_rd[smb] = []
                Eb.read(ei, tpv)
                return tpv

            def fin_A(Qb, tpv, stt):
                ob, smb = blk[Qb]
                Ov = psum[:, ob, :].rearrange("p (m q) -> p m q", m=2)
                Smv = psum[:, smb, :].rearrange("p (m q) -> p m q", m=2)
                ri, rS, rdeps = rSb.next()
                t1 = B.op("act", lambda e, rS=rS, Smv=Smv: e.activation(rS[:], Smv[:], AF.Ln), deps=rdeps + [tpv])
                bank_rd[smb].append(t1)
                t2 = B.op("act", lambda e, rS=rS: e.activation(rS[:], rS[:], AF.Exp, scale=-1.0), deps=[t1])
                t3 = B.op("dve", lambda e, rS=rS, Ov=Ov: e.tensor_tensor(rS[:], Ov[:], rS[:], ALU.mult), deps=[t2, tpv])
                bank_rd[ob].append(t3)
                oi, od, odeps = odb.next()
                t4 = B.op("dve", lambda e, od=od, rS=rS: e.scalar_tensor_tensor(od[:], rS[:, 1, :], neg_lam, rS[:, 0, :], ALU.mult, ALU.add),
                          deps=odeps + [t3, t_lam])
                rSb.wrote(ri, t4)
                stt.update(oi=oi, od=od, t4=t4)

            def fin_B(Qb, stt):
                od, t4 = stt["od"], stt["t4"]
                qi, sqv, qdeps = sqb.next()
                t5 = B.op("dve", lambda e, sqv=sqv, od=od: e.tensor_tensor(sqv[:], od[:], od[:], ALU.mult), deps=qdeps + [t4])
                _, pb, _ = PbC.next()
                t6 = mm_group(psum[:, pb, 0:256], [(ones_b[:], sqv[:])], [t5], pb)
                sqb.read(qi, t6)
                stt.update(pb=pb, t6=t6)

            def fin_C(Qb, stt):
                od, t4, pb, t6, oi = stt["od"], stt["t4"], stt["pb"], stt["t6"], stt["oi"]
                q0b = Qb * 256
                di, rd, ddeps = rdb.next()
                t7 = B.op("act", lambda e, rd=rd, pb=pb: e.activation(rd[:], psum[:, pb, 0:256], AF.Ln, bias=EPS, scale=1.0 / 128),
                          deps=ddeps + [t6])
                bank_rd[pb].append(t7)
                t8 = B.op("act", lambda e, rd=rd: e.activation(rd[:], rd[:], AF.Exp, scale=-0.5), deps=[t7])
                t9 = B.op("dve", lambda e, od=od, rd=rd, h=h, q0b=q0b: e.scalar_tensor_tensor(
                    bufA[:, h, q0b:q0b + 256], od[:], gsub8, rd[:], ALU.mult, ALU.mult), deps=[t8, t4, t_lam])
                odb.read(oi, t9)
                odb.wrote(oi, t4)
                rdb.wrote(di, t8)
                rdb.read(di, t9)

            sched = {}

            def at(i, fn):
                sched.setdefault(i, []).append(fn)

            PF = 2
            sq_ = [emit_S(pairs[k]) for k in range(PF)]
            last_tpv = None
            npairs = len(pairs)
            for i, pair in enumerate(pairs):
                if i + PF < npairs:
                    sq_.append(emit_S(pairs[i + PF]))
                cur = sq_.pop(0)
                tpv = emit_PV(pair, cur)
                last_tpv = tpv
                Qb, j = pair
                if j == 2 * Qb + 1:
                    stt = {}
                    e2 = i + 4 * Qb + 10 if Qb < 6 else 10 ** 9
                    at(i + 1, lambda Qb=Qb, tpv=tpv, stt=stt: fin_A(Qb, tpv, stt))
                    at(min(i + 12, e2 - 1), lambda Qb=Qb, stt=stt: fin_B(Qb, stt))
                    at(min(i + 16, e2), lambda Qb=Qb, stt=stt: fin_C(Qb, stt))
                for fn in sched.pop(i, []):
                    fn()
            for i in sorted(sched):
                for fn in sched[i]:
                    fn()
            q_rd.append(last_tpv)
            k_rd.append(last_tpv)
            v_rd.append(last_tpv)
            if stage == 'head0':
                bt = B.barrier(); finish(bt); B.emit(); return nc

        def branch_out(pname, wname, src, mbuf, sgbuf, bt, epilogue=None):
            rots = (Rot([0, 1]), Rot([2, 3]), Rot([4, 5, 6, 7]))
            m_tok = {}
            sg_rd = list(bt)
            for i in range(4):
                slot, lt = w_get((pname, i))
                wp = slot[:, 0:2048].rearrange("p (k n) -> p k n", k=8)
                wgt = slot[:, 2048:4096].rearrange("p (k n) -> p k n", k=8)
                last = None
                for T in range(NT):
                    tsl = slice(T * TB, (T + 1) * TB)
                    for dd in range(2):
                        d = 2 * i + dd
                        _, yb, _ = rots[0].next()
                        _, gb, _ = rots[1].next()
                        ty = mm_group(psum[:, yb, :], [(wp[:, kc, dd * 128:(dd + 1) * 128], src[:, kc, tsl]) for kc in range(8)],
                                      [lt] + bt, yb)
                        tg = mm_group(psum[:, gb, :], [(wgt[:, kc, dd * 128:(dd + 1) * 128], hT[:, kc, tsl]) for kc in range(8)],
                                      [lt], gb)
                        last = tg
                        ta = B.op("act", lambda e, gb=gb: e.activation(sgbuf[:], psum[:, gb, :], AF.Sigmoid), deps=[tg] + sg_rd)
                        bank_rd[gb].append(ta)
                        tm = B.op("dve", lambda e, d=d, yb=yb, tsl=tsl: e.tensor_tensor(mbuf[:, d, tsl], sgbuf[:], psum[:, yb, :], ALU.mult),
                                  deps=[ta, ty] + bt)
                        bank_rd[yb].append(tm)
                        sg_rd = [tm]
                        m_tok[(d, T)] = tm
                w_release((pname, i), [last])
            last_proj = last
            for i in range(2):
                slot, lt = w_get((wname, i))
                wo = slot[:, 0:4096].rearrange("p (k n) -> p k n", k=8)
                last = None
                for T in range(NT):
                    tsl = slice(T * TB, (T + 1) * TB)
                    for dd in range(4):
                        d = 4 * i + dd
                        _, ob, _ = rots[2].next()
                        to = mm_group(psum[:, ob, :], [(wo[:, kc, dd * 128:(dd + 1) * 128], mbuf[:, kc, tsl]) for kc in range(8)],
                                      [lt, m_tok[(7, T)]], ob)
                        last = to
                        tx = B.op("dve", lambda e, d=d, ob=ob, tsl=tsl: e.scalar_tensor_tensor(
                            xT[:, d, tsl], psum[:, ob, :], g1[:, d:d + 1], xT[:, d, tsl], ALU.mult, ALU.add),
                            deps=[to, mod_tok[1]] + bt)
                        bank_rd[ob].append(tx)
                    if epilogue is not None and i == 1:
                        epilogue(T, [tx], [last_proj])
                w_release((wname, i), [last])

        bt = B.barrier()
        po[0] = U_OFF
        mbuf = palloc(BF16, 8 * S).rearrange("p (c t) -> p c t", c=8)
        sgbuf = palloc(F32, TB)
        assert po[0] <= ARENA, po[0]
        branch_out("aproj", "woa", bufA, mbuf, sgbuf, bt)

        if stage == "attn":
            bt = B.barrier()
            finish(bt)
            B.emit()
            return nc

        bt = B.barrier()
        po[0] = U_OFF
        cBs = palloc(F32, NB)
        wTm = palloc(BF16, 1024).rearrange("p (g t) -> p g t", g=8)
        gfb = Rot([palloc(F32, 1024) for _ in range(2)])
        vnb = Rot([palloc(BF16, 1024) for _ in range(2)])
        tmpb = Rot([palloc(F32, TB) for _ in range(2)])
        stbb = Rot([palloc(F32, 16) for _ in range(2)])
        assert po[0] <= ARENA, po[0]
        gu = bufA
        s_cb = B.new_sem("cB")
        t_cB = B.dma("sp", cBs[:], cB_d, s_cb, deps=bt)
        tri = cBs[:, B_TRI:B_TRI + 128]
        wst = cBs[:, B_WST:B_WST + 1024].rearrange("p (g t) -> p g t", g=8)
        lng_c = cBs[:, B_LNG:B_LNG + 8]
        lnb_c = cBs[:, B_LNB:B_LNB + 8]
        bs_bc = cBs[:, B_BS:B_BS + 1024]
        t_w = None
        for g in range(8):
            t_w = B.op("dve", lambda e, g=g: e.tensor_tensor(wTm[:, g, :], wst[:, g, :], tri, ALU.mult), deps=[t_cB] + bt)
        t_bg = None
        for half in range(2):
            trs = mm_group(psum[:, 6 + half, :], [(ones_b[:], wTm[:, 4 * half:4 * half + 4, :])], [t_w] + bt, 6 + half)
            for gg in range(4):
                g = 4 * half + gg
                t_bg = B.op("dve", lambda e, g=g, gg=gg, half=half: e.scalar_tensor_tensor(
                    bs_bc[:, g * 128:(g + 1) * 128], psum[:, 6 + half, gg * 128:(gg + 1) * 128], lnb_c[:, g:g + 1],
                    bs_bc[:, g * 128:(g + 1) * 128], ALU.mult, ALU.add), deps=[trs, t_cB])
            bank_rd[6 + half].append(t_bg)
        u_tok = {}
        Ub = Rot([0, 1, 2, 3])
        for i in range(2):
            slot, lt = w_get(("u", i))
            w = slot[:, 0:4096].rearrange("p (k n) -> p k n", k=8)
            last = None
            for T in range(NT):
                tsl = slice(T * TB, (T + 1) * TB)
                for cc in range(4):
                    c = 4 * i + cc
                    _, ub, _ = Ub.next()
                    tu = mm_group(psum[:, ub, :], [(w[:, kc, cc * 128:(cc + 1) * 128], hT[:, kc, tsl]) for kc in range(8)],
                                  [lt] + bt, ub)
                    last = tu
                    ta = B.op("act", lambda e, c=c, ub=ub, tsl=tsl: e.activation(gu[:, c, tsl], psum[:, ub, :], AF.Gelu), deps=[tu] + bt)
                    bank_rd[ub].append(ta)
                    u_tok[(c, T)] = ta
            w_release(("u", i), [last])
        slot0, lt0 = w_get(("gv", 0))
        slot1, lt1 = w_get(("gv", 1))
        wgv = [slot0[:, 0:4096].rearrange("p (k n) -> p k n", k=8), slot1[:, 0:4096].rearrange("p (k n) -> p k n", k=8)]
        Gb, Fb = Rot([0, 1, 2, 3]), Rot([4, 5, 6, 7])
        last_gv = None

        def gv_A(tt):
            nonlocal last_gv
            t0 = tt * 128
            gi, gfa, gdeps = gfb.next()
            si, sta, sdeps = stbb.next()
            tstat = []
            for half in range(2):
                _, gb, _ = Gb.next()
                tg = mm_group(psum[:, gb, :], [(hT[:, kc, t0:t0 + 128], wgv[half][:, kc, :]) for kc in range(8)], [lt0, lt1] + bt, gb)
                last_gv = tg
                ta = B.op("act", lambda e, half=half, gb=gb, gfa=gfa: e.activation(gfa[:, half * 512:(half + 1) * 512], psum[:, gb, :], AF.Gelu),
                          deps=[tg] + gdeps)
                bank_rd[gb].append(ta)
                tstat.append(B.op("dve", lambda e, half=half, gfa=gfa, sta=sta: e.bn_stats(sta[:, half * 6:(half + 1) * 6], gfa[:, half * 512:(half + 1) * 512]),
                                  deps=[ta] + sdeps))
            t = B.op("dve", lambda e, sta=sta: e.bn_aggr(sta[:, 12:14], sta[:, 0:12]), deps=tstat)
            t = B.op("act", lambda e, sta=sta: e.activation(sta[:, 14:15], sta[:, 13:14], AF.Ln, bias=EPS), deps=[t])
            t = B.op("act", lambda e, sta=sta: e.activation(sta[:, 14:15], sta[:, 14:15], AF.Exp, scale=-0.5), deps=[t])
            t = B.op("dve", lambda e, sta=sta: e.scalar_tensor_tensor(sta[:, 15:16], sta[:, 12:13], -1.0, sta[:, 14:15], ALU.mult, ALU.mult), deps=[t])
            vi, vn, vdeps = vnb.next()
            tvn = B.op("act", lambda e, gfa=gfa, sta=sta, vn=vn: e.activation(vn[:], gfa[:], AF.Identity, bias=sta[:, 15:16], scale=sta[:, 14:15]),
                       deps=[t] + vdeps)
            stbb.wrote(si, tvn)
            gfb.wrote(gi, tvn)
            vnb.wrote(vi, tvn)
            return (vi, vn, tvn)

        def gv_B(tt, info):
            vi, vn, tvn = info
            t0 = tt * 128
            for fi in range(2):
                _, fb, _ = Fb.next()
                tf = None
                for gg in range(4):
                    g = fi * 4 + gg
                    tf = mm_group(psum[:, fb, gg * 128:(gg + 1) * 128], [(vn[:, g * 128:(g + 1) * 128], wTm[:, g, :])],
                                  [tvn, t_w], fb, start=True)
                vnb.read(vi, tf)
                pi, tp, pdeps = tmpb.next()
                t1 = None
                for gg in range(4):
                    g = fi * 4 + gg
                    t1 = B.op("dve", lambda e, tp=tp, fb=fb, g=g, gg=gg: e.scalar_tensor_tensor(
                        tp[:, gg * 128:(gg + 1) * 128], psum[:, fb, gg * 128:(gg + 1) * 128], lng_c[:, g:g + 1],
                        bs_bc[:, g * 128:(g + 1) * 128], ALU.mult, ALU.add), deps=[tf, t_cB, t_bg] + pdeps)
                bank_rd[fb].append(t1)
                t2 = B.op("dve", lambda e, tp=tp, fi=fi, t0=t0: e.tensor_tensor(
                    gu[:, 4 * fi:4 * fi + 4, t0:t0 + 128], gu[:, 4 * fi:4 * fi + 4, t0:t0 + 128],
                    tp[:].rearrange("p (g t) -> p g t", g=4), ALU.mult),
                    deps=[t1] + [u_tok[(4 * fi + gg, tt // 4)] for gg in range(4)])
                tmpb.wrote(pi, t2)

        nxt = gv_A(0)
        for tt in range(16):
            cur = nxt
            if tt + 1 < 16:
                nxt = gv_A(tt + 1)
            gv_B(tt, cur)
        w_release(("gv", 0), [last_gv])
        w_release(("gv", 1), [last_gv])

        bt = B.barrier()
        po[0] = U_OFF
        mbuf = palloc(BF16, 8 * S).rearrange("p (c t) -> p c t", c=8)
        sgbuf = palloc(F32, TB)
        nb2 = make_norm(modT[:, 48:56], der[:, 24:32], 7, P_OFF, nsq=8)
        h_tok2 = [None] * NT
        pend2 = []

        def flush2():
            while pend2:
                T, sqs, xtoks, hrd = pend2.pop(0)
                info = nb2(T, xtoks + hrd, [], [], phase="st", info=sqs)
                h_tok2[T] = nb2(T, xtoks + hrd, [mod_tok[2]], hrd, phase="apply", info=info)

        def epi2(T, xtoks, hrd):
            flush2()
            pend2.append((T, nb2(T, xtoks + hrd, [], [], phase="sq"), xtoks, hrd))

        branch_out("bproj", "wob", gu, mbuf, sgbuf, bt, epilogue=(None if stage == "mixer" else epi2))
        flush2()

        if stage == "mixer":
            bt = B.barrier()
            finish(bt)
            B.emit()
            return nc

        bt = B.barrier()
        po[0] = P_OFF + 20480
        _, out_toks = ffn_phase("ffn2", h_tok2, der[:, 32:40], None, True)
        B.wait_only("sp", [(s_out, B.dma_cnt[s_out])])
        B.emit()
    return nc


_CACHE = {}


def kernel(**inp):
    x = np.asarray(inp["x"], np.float32)
    c = np.asarray(inp["c"], np.float32)
    inp = {k: np.asarray(v) for k, v in inp.items()}
    wblob = build_wblob(inp)
    cA, cB, cC = build_consts(inp)
    if "nc" not in _CACHE:
        _CACHE["nc"] = build_program("full")
    nc = _CACHE["nc"]
    in_maps = []
    for b in range(8):
        in_maps.append({"xT": np.ascontiguousarray(x[b].T), "cv": np.ascontiguousarray(c[b].reshape(8, 128).T),
                        "cA": cA, "cB": cB, "cC": cC, "wblob": wblob})
    res = run_bass_kernel_spmd(nc, in_maps, core_ids=list(range(8)))
    out = np.stack([np.asarray(res.results[b]["outT"]).T for b in range(8)])
    return np.ascontiguousarray(out.astype(np.float32))
```

```python
import math
import numpy as np
from contextlib import ExitStack
import concourse.bass as bass
import concourse.mybir as mybir
from concourse.bass_utils import run_bass_kernel_spmd

F32 = mybir.dt.float32
BF16 = mybir.dt.bfloat16
U8 = mybir.dt.uint8
AF = mybir.ActivationFunctionType
ALU = mybir.AluOpType

D = 1024
S = 2048
DFF = 2816
NCH = 8
NT = 4
TB = 512
H = 8
EPS = 1e-6
COL_Q, COL_K, COL_V, COL_U, COL_GV, COL_GATE = 0, 1024, 2048, 3072, 4096, 5120
SLOT = 6144
NSLOT = 3
NFG = 11
LAM_INIT = 0.8 - 0.6 * math.exp(-0.3 * 0)
SEM_LIMIT = 30000

A_BADA = 0
A_GQ = 72
A_GK = 73
A_GSUB = 74
A_LAM = 75
NA = A_LAM + 256
B_TRI = 0
B_WST = 128
B_LNG = B_WST + 1024
B_LNB = B_LNG + 8
B_BS = B_LNB + 8
NB = B_BS + 1024


class Builder:
    ENGS = ("pe", "act", "dve", "pool", "sp")

    def __init__(self, nc, stack):
        self.nc = nc
        self.stack = stack
        self.ops = {e: [] for e in self.ENGS}
        self.sems = []
        self.cur = {}
        self.waited = {e: {} for e in self.ENGS}
        for e in self.ENGS:
            self.cur[e] = [self.new_sem("prog_" + e), 0]
        self.dma_cnt = {}

    def new_sem(self, name):
        h = self.stack.enter_context(self.nc.semaphore(name + "_%d" % len(self.sems)))
        self.sems.append(h)
        return len(self.sems) - 1

    def _waits(self, eng, deps):
        waits = []
        for t in deps:
            if t is None:
                continue
            s, v = t
            if self.waited[eng].get(s, 0) >= v:
                continue
            self.waited[eng][s] = v
            waits.append((s, v))
        return waits

    def op(self, eng, fn, deps=(), signal=True):
        waits = self._waits(eng, deps)
        inc = None
        tok = None
        if signal:
            cur = self.cur[eng]
            if cur[1] >= SEM_LIMIT:
                cur[0] = self.new_sem("prog_" + eng)
                cur[1] = 0
            cur[1] += 1
            inc = (cur[0], 1)
            tok = (cur[0], cur[1])
        self.ops[eng].append((waits, fn, inc))
        return tok

    def dma(self, eng, out, in_, sem, deps=()):
        waits = self._waits(eng, deps)
        self.dma_cnt[sem] = self.dma_cnt.get(sem, 0) + 16
        self.ops[eng].append((waits, lambda e: e.dma_start(out=out, in_=in_), (sem, 16)))
        return (sem, self.dma_cnt[sem])

    def wait_only(self, eng, deps):
        waits = self._waits(eng, deps)
        if waits:
            self.ops[eng].append((waits, None, None))

    def last(self, eng):
        c = self.cur[eng]
        return (c[0], c[1]) if c[1] > 0 else None

    def barrier(self):
        toks = {e: self.last(e) for e in ("pe", "act", "dve")}
        for e in ("pe", "act", "dve"):
            self.wait_only(e, [toks[o] for o in ("pe", "act", "dve") if o != e])
        return [t for t in toks.values() if t is not None]

    def emit(self):
        nc = self.nc
        with nc.Block() as block:
            def body(name):
                def f(e):
                    for waits, fn, inc in self.ops[name]:
                        for s, v in waits:
                            e.wait_ge(self.sems[s], v)
                        if fn is None:
                            continue
                        ins = fn(e)
                        if inc is not None:
                            ins.then_inc(self.sems[inc[0]], inc[1])
                return f
            block.tensor(body("pe"))
            block.scalar(body("act"))
            block.vector(body("dve"))
            block.gpsimd(body("pool"))
            block.sync(body("sp"))


class Rot:
    def __init__(self, aps):
        self.aps = aps
        self.i = -1
        self.readers = [[] for _ in aps]
        self.writer = [None for _ in aps]

    def next(self):
        self.i = (self.i + 1) % len(self.aps)
        deps = list(self.readers[self.i])
        if self.writer[self.i] is not None:
            deps.append(self.writer[self.i])
        self.readers[self.i] = []
        self.writer[self.i] = None
        return self.i, self.aps[self.i], deps

    def wrote(self, i, tok):
        self.writer[i] = tok

    def read(self, i, tok):
        self.readers[i].append(tok)


def _pk(W, c0, c1):
    n = c1 - c0
    return np.ascontiguousarray(W[:, c0:c1].reshape(8, 128, n).transpose(1, 0, 2).reshape(128, 8 * n))


def slice_plan():
    order = []
    for i in range(4):
        order.append(("ada", i))
    order.append(("ffn1", 0))
    order.append(("ada", 4))
    order.append(("ada", 5))
    nxt = 6
    for g in range(1, NFG):
        order.append(("ffn1", g))
        if nxt < 18:
            order.append(("ada", nxt))
            nxt += 1
    while nxt < 18:
        order.append(("ada", nxt))
        nxt += 1
    for h in range(H):
        order.append(("att", h))
    for i in range(4):
        order.append(("aproj", i))
    for i in range(2):
        order.append(("woa", i))
    for i in range(2):
        order.append(("u", i))
    for i in range(2):
        order.append(("gv", i))
    for i in range(4):
        order.append(("bproj", i))
    for i in range(2):
        order.append(("wob", i))
    for g in range(NFG):
        order.append(("ffn2", g))
    return order


def slice_size(kind):
    return {"ada": 4096, "ffn1": 6144, "ffn2": 6144, "att": 3072, "aproj": 4096, "woa": 4096,
            "u": 4096, "gv": 4096, "bproj": 4096, "wob": 4096}[kind]


def build_wblob(inp):
    w_ada = inp["w_ada"][0]
    w_in = inp["w_in"][0]
    parts = []
    for kind, i in slice_plan():
        if kind == "ada":
            a = _pk(w_ada, 512 * i, 512 * i + 512)
        elif kind in ("ffn1", "ffn2"):
            if kind == "ffn1":
                Wg, Wu, Wd = inp["w_ffn1_gate"][0], inp["w_ffn1_up"][0], inp["w_ffn1_down"][0]
            else:
                Wg, Wu, Wd = inp["w_ffn2_gate"][0], inp["w_ffn2_up"][0], inp["w_ffn2_down"][0]
            wd = Wd[256 * i:256 * i + 256, :].reshape(2, 128, 1024).transpose(1, 0, 2).reshape(128, 2048)
            a = np.concatenate([_pk(Wg, 256 * i, 256 * i + 256), _pk(Wu, 256 * i, 256 * i + 256), wd], axis=1)
        elif kind == "att":
            a = np.concatenate([_pk(w_in, COL_Q + 128 * i, COL_Q + 128 * i + 128),
                                _pk(w_in, COL_K + 128 * i, COL_K + 128 * i + 128),
                                _pk(w_in, COL_V + 128 * i, COL_V + 128 * i + 128)], axis=1)
        elif kind == "aproj":
            a = np.concatenate([_pk(inp["w_a_proj"][0], 256 * i, 256 * i + 256),
                                _pk(w_in, COL_GATE + 256 * i, COL_GATE + 256 * i + 256)], axis=1)
        elif kind == "bproj":
            a = np.concatenate([_pk(inp["w_b_proj"][0], 256 * i, 256 * i + 256),
                                _pk(w_in, COL_GATE + 1024 + 256 * i, COL_GATE + 1024 + 256 * i + 256)], axis=1)
        elif kind in ("woa", "wob"):
            a = _pk(inp["w_o"][0], 512 * i, 512 * i + 512)
        elif kind == "u":
            a = _pk(w_in, COL_U + 512 * i, COL_U + 512 * i + 512)
        elif kind == "gv":
            a = _pk(w_in, COL_GV + 512 * i, COL_GV + 512 * i + 512)
        assert a.shape == (128, slice_size(kind)), (kind, a.shape)
        parts.append(a.astype(np.float32, copy=False))
    return np.ascontiguousarray(np.concatenate(parts, axis=1))


def rel_bucket_np(n):
    n = np.asarray(n)
    max_exact = 16
    nf = np.maximum(n, 1).astype(np.float32)
    large = max_exact + (np.log(nf / np.float32(max_exact)) / np.float32(math.log(128 / max_exact))
                         * np.float32(32 - max_exact)).astype(np.int32)
    large = np.minimum(large, 31)
    return np.where(n < max_exact, n, large)


def build_consts(inp):
    cA = np.zeros((128, NA), np.float32)
    cA[:, A_BADA:A_BADA + 72] = inp["b_ada"][0].reshape(72, 128).T
    cA[:, A_GQ] = np.tile(inp["q_norm_g"][0], 2)
    cA[:, A_GK] = np.tile(inp["k_norm_g"][0], 2)
    cA[:, A_GSUB] = inp["subln_g"][0]
    lam = np.concatenate([inp["lam_q1"][0], inp["lam_k1"][0], inp["lam_q2"][0], inp["lam_k2"][0]])
    cA[:, A_LAM:A_LAM + 256] = lam[None, :]
    cB = np.zeros((128, NB), np.float32)
    s = np.arange(128)
    cB[:, B_TRI:B_TRI + 128] = (s[:, None] <= s[None, :]).astype(np.float32)
    cB[:, B_WST:B_WST + 1024] = inp["w_spatial"][0].transpose(2, 0, 1).reshape(128, 1024)
    cB[:, B_LNG:B_LNG + 8] = inp["gmlp_ln_g"][0].reshape(8, 128).T
    cB[:, B_LNB:B_LNB + 8] = inp["gmlp_ln_b"][0].reshape(8, 128).T
    cB[:, B_BS:B_BS + 1024] = inp["b_spatial"][0].reshape(1, 1024)
    cC = np.zeros((32, 8 + 256), np.float32)
    cC[:, 0:8] = inp["rel_bias_table"]
    bk = rel_bucket_np(np.arange(256))
    oh = np.zeros((32, 256), np.float32)
    oh[bk, np.arange(256)] = 1.0
    oh[31, :] -= 1.0
    cC[:, 8:] = oh
    return cA, cB, cC


def build_program(stage="full"):
    nc = bass.Bass("TRN2", target_bir_lowering=False)
    plan = slice_plan()
    offs = []
    o = 0
    for kind, i in plan:
        offs.append(o)
        o += slice_size(kind)
    WTOT = o
    sidx = {k: j for j, k in enumerate(plan)}

    xT_d = nc.dram_tensor("xT", [D, S], F32, kind="ExternalInput").ap()
    cv_d = nc.dram_tensor("cv", [128, 8], F32, kind="ExternalInput").ap()
    cA_d = nc.dram_tensor("cA", [128, NA], F32, kind="ExternalInput").ap()
    cB_d = nc.dram_tensor("cB", [128, NB], F32, kind="ExternalInput").ap()
    cC_d = nc.dram_tensor("cC", [32, 264], F32, kind="ExternalInput").ap()
    wb_d = nc.dram_tensor("wblob", [128, WTOT], F32, kind="ExternalInput").ap()
    out_d = nc.dram_tensor("outT", [D, S], F32, kind="ExternalOutput").ap()
    scr_h = nc.dram_tensor("ebscr", [8, 384], BF16, kind="Internal")
    xT_v = xT_d.rearrange("(c p) t -> p c t", p=128)
    out_v = out_d.rearrange("(c p) t -> p c t", p=128)

    with ExitStack() as st:
        B = Builder(nc, st)
        ARENA = 212000
        arena = st.enter_context(nc.sbuf_tensor("arena", [128, ARENA], U8))
        psum = st.enter_context(nc.psum_tensor("psum", [128, 8, 512], F32))

        def view(off, dtype, n):
            nb = n * (4 if dtype == F32 else 2)
            assert off % 4 == 0 and off + nb <= ARENA, (off, nb)
            return arena[:, off:off + nb].bitcast(dtype)

        X_OFF, H_OFF, W_OFF, C_OFF, P_OFF = 0, 65536, 98304, 98304 + NSLOT * SLOT * 2, 98304 + NSLOT * SLOT * 2 + 8192
        xT = view(X_OFF, F32, 8 * S).rearrange("p (c t) -> p c t", c=8)
        hT = view(H_OFF, BF16, 8 * S).rearrange("p (c t) -> p c t", c=8)
        slots = [view(W_OFF + i * SLOT * 2, BF16, SLOT) for i in range(NSLOT)]
        co = [C_OFF]

        def calloc(dtype, n):
            v = view(co[0], dtype, n)
            co[0] += ((n * (4 if dtype == F32 else 2) + 3) // 4) * 4
            assert co[0] <= P_OFF
            return v
        cA = calloc(F32, NA)
        modT = calloc(F32, 72)
        der = calloc(F32, 48)
        cvs = calloc(F32, 8)
        sc_bf = calloc(BF16, 8)
        ones_f = calloc(F32, 128)
        bo_f = calloc(F32, 128)
        ones_b = calloc(BF16, 128)
        EB2 = calloc(BF16, 2048).rearrange("p (h q) -> p h q", h=8)
        EBd = EB2[:, :, 0:128]
        EBs = EB2[:, :, 128:256]
        lamw = calloc(F32, 16)
        so = ARENA - 4096
        cC = view(so, F32, 264)
        ebrow = view(so + 1056, BF16, 384)
        lamtmp = view(so + 1056 + 768, F32, 128)

        po = [P_OFF]

        def preset():
            po[0] = P_OFF

        def palloc(dtype, n):
            v = view(po[0], dtype, n)
            po[0] += ((n * (4 if dtype == F32 else 2) + 3) // 4) * 4
            return v

        wsem = [B.new_sem("wslot") for _ in range(NSLOT)]
        W = {"next": 0, "load": {}, "rel": {}}

        def w_ensure(i):
            while W["next"] <= i and W["next"] < len(plan):
                j = W["next"]
                assert j < NSLOT or (j - NSLOT) in W["rel"], (j, plan[j])
                deps = W["rel"].get(j - NSLOT, []) if j >= NSLOT else []
                if j == 0:
                    deps = list(deps) + [x_tok[0], x_tok[1], x_tok[2]]
                n = slice_size(plan[j][0])
                W["load"][j] = B.dma("pool", slots[j % NSLOT][:, 0:n], wb_d[:, offs[j]:offs[j] + n],
                                     wsem[j % NSLOT], deps)
                W["next"] += 1

        def w_get(key):
            i = sidx[key]
            w_ensure(i)
            for j in range(i + 1, i + NSLOT):
                if j - NSLOT < 0 or (j - NSLOT) in W["rel"]:
                    w_ensure(j)
                else:
                    break
            return slots[i % NSLOT], W["load"][i]

        def w_release(key, toks):
            W["rel"][sidx[key]] = list(toks)

        bank_rd = [[] for _ in range(8)]

        def mm_group(out_ap, pairs, deps, bank, start=True, stop=True, sig=True):
            tok = None
            n = len(pairs)
            for i, (l, r) in enumerate(pairs):
                d = list(deps) + bank_rd[bank] if i == 0 else ()
                if i == 0 and start:
                    bank_rd[bank] = []
                tok = B.op("pe", lambda e, l=l, r=r, s0=(start and i == 0), s1=(stop and i == n - 1):
                           e.matmul(out_ap, l, r, start=s0, stop=s1),
                           deps=d, signal=(sig and i == n - 1))
            return tok

        s_c = B.new_sem("cload")
        t_cA = B.dma("sp", cA[:], cA_d, s_c)
        t_cv = B.dma("sp", cvs[:], cv_d, s_c)
        t_cC = B.dma("sp", cC[0:32, :], cC_d, s_c)
        t_cl = (s_c, 48)
        xsem = [B.new_sem("xload") for _ in range(NT)]
        x_tok = [B.dma("sp", xT[:, :, T * TB:(T + 1) * TB], xT_v[:, :, T * TB:(T + 1) * TB], xsem[T]) for T in range(NT)]

        t = B.op("dve", lambda e: e.memset(ones_f[:], 1.0))
        t = B.op("dve", lambda e: e.memset(bo_f[:], 0.0))
        t = B.op("dve", lambda e: e.memset(bo_f[0:64, 0:64], 1.0), deps=[t])
        t = B.op("dve", lambda e: e.memset(bo_f[64:128, 64:128], 1.0), deps=[t])
        t = B.op("dve", lambda e: e.memset(ones_b[:], 1.0))
        t_ebz = B.op("dve", lambda e: e.memset(ebrow[0:8, :], 0.0))
        t_const = t_ebz
        t_sc = B.op("act", lambda e: e.activation(sc_bf[:], cvs[:], AF.Silu), deps=[t_cl])

        lv = cA[:, A_LAM:A_LAM + 256]
        t1 = B.op("dve", lambda e: e.tensor_tensor(lamtmp[:, 0:64], lv[:, 0:64], lv[:, 64:128], ALU.mult), deps=[t_cl])
        t2 = B.op("dve", lambda e: e.tensor_tensor(lamtmp[:, 64:128], lv[:, 128:192], lv[:, 192:256], ALU.mult), deps=[t_cl])
        t1 = B.op("dve", lambda e: e.reduce_sum(lamw[:, 0:1], lamtmp[:, 0:64], mybir.AxisListType.X), deps=[t1])
        t2 = B.op("dve", lambda e: e.reduce_sum(lamw[:, 1:2], lamtmp[:, 64:128], mybir.AxisListType.X), deps=[t2])
        t3 = B.op("act", lambda e: e.activation(lamw[:, 2:4], lamw[:, 0:2], AF.Exp), deps=[t1, t2])
        t4 = B.op("dve", lambda e: e.tensor_tensor(lamw[:, 4:5], lamw[:, 3:4], lamw[:, 2:3], ALU.subtract), deps=[t3])
        t4 = B.op("dve", lambda e: e.tensor_scalar(lamw[:, 4:5], lamw[:, 4:5], -LAM_INIT, None, ALU.add), deps=[t4])
        t5 = B.op("dve", lambda e: e.tensor_scalar(lamw[:, 5:6], cA[:, A_GSUB:A_GSUB + 1], 1.0 - LAM_INIT, None, ALU.mult), deps=[t_cl])
        t_lam = t5
        neg_lam = lamw[:, 4:5]
        gsub8 = lamw[:, 5:6]

        tb = mm_group(psum[0:8, 7, 0:256], [(cC[0:32, 0:8], cC[0:32, 8:264])], [t_cl], 7)
        te = B.op("act", lambda e: e.activation(ebrow[0:8, 127:383], psum[0:8, 7, 0:256], AF.Exp), deps=[tb, t_ebz])
        bank_rd[7].append(te)
        s_eb = B.new_sem("eb")
        tw = B.dma("sp", scr_h.ap(), ebrow[0:8, :], s_eb, deps=[te])
        for k in range(128):
            B.dma("sp", EB2[k:k + 1, :, :], bass.AP(scr_h, 127 - k, [[0, 1], [384, 8], [1, 256]]), s_eb, deps=[tw])
        t_eb = (s_eb, B.dma_cnt[s_eb])

        modps = psum[:, 7, 256:328]
        mod_tok = {}

        def ada_slice(i):
            slot, lt = w_get(("ada", i))
            w = slot[:, 0:4096].rearrange("p (k n) -> p k n", k=8)
            tok = None
            for sub in range(4):
                jc = 4 * i + sub
                tok = mm_group(modps[:, jc:jc + 1],
                               [(w[:, kc, sub * 128:(sub + 1) * 128], sc_bf[:, kc:kc + 1]) for kc in range(8)],
                               [lt, t_sc], 7)
            w_release(("ada", i), [tok])
            return tok

        hg_dep = []

        def mod_finish0(part, tok):
            a, b = (0, 16) if part == 0 else (16, 24)
            t = B.op("dve", lambda e: e.tensor_tensor(modT[:, a:b], modps[:, a:b], cA[:, A_BADA + a:A_BADA + b], ALU.add),
                     deps=[tok, t_cl])
            if part == 0:
                t2 = B.op("dve", lambda e: e.tensor_scalar(der[:, 0:8], modT[:, 8:16], 1.0, None, ALU.add), deps=[t])
                mod_tok[0] = t2
            else:
                t2 = B.op("dve", lambda e: e.tensor_scalar(der[:, 8:16], modT[:, 16:24], 0.5, None, ALU.mult), deps=[t])
                hg_dep.append(t2)
            return t2

        def mod_finish(k, tok):
            a, b = 24 * k, 24 * k + 24
            t = B.op("dve", lambda e: e.tensor_tensor(modT[:, a:b], modps[:, a:b], cA[:, A_BADA + a:A_BADA + b], ALU.add),
                     deps=[tok, t_cl])
            dcol = {0: 0, 1: 16, 2: 24}[k]
            t2 = B.op("dve", lambda e: e.tensor_scalar(der[:, dcol:dcol + 8], modT[:, a + 8:a + 16], 1.0, None, ALU.add), deps=[t])
            if k != 1:
                hcol = 8 if k == 0 else 32
                t2 = B.op("dve", lambda e: e.tensor_scalar(der[:, hcol:hcol + 8], modT[:, a + 16:a + 24], 0.5, None, ALU.mult), deps=[t])
            mod_tok[k] = t2
            return t2

        def make_norm(sh, sc1p, statbank, off, nrs=2, nsq=2):
            save = po[0]
            po[0] = off
            sq = Rot([palloc(BF16, TB) for _ in range(nsq)])
            lb = palloc(F32, TB)
            tmp = Rot([palloc(F32, TB) for _ in range(2)])
            rs = Rot([palloc(F32, TB) for _ in range(nrs)])
            po[0] = save
            stt = {"lb_rd": []}

            def block(T, xdeps, moddeps, hwar, phase="both", info=None):
                tsl = slice(T * TB, (T + 1) * TB)
                if phase == "apply":
                    i, rap, tr = info
                    return apply_(T, tsl, i, rap, tr, xdeps, moddeps, hwar)
                tok = None
                if phase == "sq":
                    out = []
                    for c in range(NCH):
                        i, ap, deps = sq.next()
                        ta = B.op("act", lambda e, ap=ap, c=c, tsl=tsl: e.activation(ap[:], xT[:, c, tsl], AF.Square),
                                  deps=deps + xdeps)
                        sq.wrote(i, ta)
                        out.append((i, ap, ta))
                    return out
                for c in range(NCH):
                    if phase == "st":
                        i, ap, ta = info[c]
                    else:
                        i, ap, deps = sq.next()
                        ta = B.op("act", lambda e, ap=ap, c=c, tsl=tsl: e.activation(ap[:], xT[:, c, tsl], AF.Square),
                                  deps=deps + xdeps)
                        sq.wrote(i, ta)
                    tok = mm_group(psum[:, statbank, :], [(ones_b[:], ap[:])], [ta, t_const], statbank,
                                   start=(c == 0), stop=(c == NCH - 1))
                    sq.read(i, tok)
                tl = B.op("act", lambda e: e.activation(lb[:], psum[:, statbank, :], AF.Ln, bias=EPS, scale=1.0 / D),
                          deps=[tok] + stt["lb_rd"] + xdeps)
                bank_rd[statbank].append(tl)
                i, rap, deps = rs.next()
                tr = B.op("act", lambda e, rap=rap: e.activation(rap[:], lb[:], AF.Exp, scale=-0.5), deps=deps + [tl])
                stt["lb_rd"] = [tr]
                rs.wrote(i, tr)
                if phase in ("stats", "st"):
                    return (i, rap, tr)
                return apply_(T, tsl, i, rap, tr, xdeps, moddeps, hwar)

            def apply_(T, tsl, i, rap, tr, xdeps, moddeps, hwar):
                th = None
                for c in range(NCH):
                    j, tap, deps = tmp.next()
                    td = B.op("dve", lambda e, tap=tap, rap=rap, c=c, tsl=tsl: e.scalar_tensor_tensor(
                        tap[:], xT[:, c, tsl], sc1p[:, c:c + 1], rap[:], ALU.mult, ALU.mult),
                        deps=deps + [tr] + xdeps + moddeps)
                    tmp.wrote(j, td)
                    rs.read(i, td)
                    th = B.op("act", lambda e, tap=tap, c=c, tsl=tsl: e.activation(hT[:, c, tsl], tap[:], AF.Identity, bias=sh[:, c:c + 1]),
                              deps=[td] + hwar + moddeps)
                    tmp.read(j, th)
                return th
            return block

        def ffn_phase(name, h_tok, hg, ada_iter, final_out, epilogue=None, early=None):
            gu_pe = {}
            sgb = Rot([palloc(F32, TB) for _ in range(2)])
            hid = Rot([palloc(BF16, 2 * TB).rearrange("p (j t) -> p j t", j=2) for _ in range(2)])
            gbanks, ubanks, dbanks = Rot([0, 1]), Rot([2, 3]), Rot([4, 5, 6, 7] if name == "ffn2" else [4, 5, 6])
            x_last = {}
            out_toks = []
            wts = {}

            def get_w(g):
                if g not in wts:
                    slot, lt = w_get((name, g))
                    wts[g] = (slot[:, 0:2048].rearrange("p (k n) -> p k n", k=8),
                              slot[:, 2048:4096].rearrange("p (k n) -> p k n", k=8),
                              slot[:, 4096:6144].rearrange("p (j n) -> p j n", j=2), lt)
                return wts[g]

            def emit_gu(g, T):
                wg, wu, wd, lt = get_w(g)
                tsl = slice(T * TB, (T + 1) * TB)
                hi, hap, hdeps = hid.next()
                hw = []
                for j in range(2):
                    _, gb, _ = gbanks.next()
                    _, ub, _ = ubanks.next()
                    tg = mm_group(psum[:, gb, :], [(wg[:, kc, j * 128:(j + 1) * 128], hT[:, kc, tsl]) for kc in range(8)],
                                  [lt, h_tok[T]], gb)
                    tu = mm_group(psum[:, ub, :], [(wu[:, kc, j * 128:(j + 1) * 128], hT[:, kc, tsl]) for kc in range(8)],
                                  [lt, h_tok[T]], ub)
                    si, sap, sdeps = sgb.next()
                    ta = B.op("act", lambda e, sap=sap, gb=gb: e.activation(sap[:], psum[:, gb, :], AF.Silu), deps=sdeps + [tg])
                    bank_rd[gb].append(ta)
                    sgb.wrote(si, ta)
                    td = B.op("dve", lambda e, hap=hap, j=j, sap=sap, ub=ub: e.tensor_tensor(hap[:, j, :], sap[:], psum[:, ub, :], ALU.mult),
                              deps=hdeps + [ta, tu])
                    bank_rd[ub].append(td)
                    sgb.read(si, td)
                    hw.append(td)
                    gu_pe[(g, T)] = tu
                hid.wrote(hi, hw[-1])
                return (hi, hap, hw)

            def emit_down(g, T, hinfo):
                wg, wu, wd, lt = get_w(g)
                hi, hap, hw = hinfo
                tsl = slice(T * TB, (T + 1) * TB)
                last_pe = None
                for d in range(NCH):
                    _, db, _ = dbanks.next()
                    tdn = mm_group(psum[:, db, :], [(wd[:, j, d * 128:(d + 1) * 128], hap[:, j, :]) for j in range(2)], hw, db)
                    hid.read(hi, tdn)
                    last_pe = tdn
                    prev = x_last.get((d, T))
                    tx = B.op("dve", lambda e, d=d, db=db, tsl=tsl: e.scalar_tensor_tensor(
                        xT[:, d, tsl], psum[:, db, :], hg[:, d:d + 1], xT[:, d, tsl], ALU.mult, ALU.add),
                        deps=[tdn, prev] + hg_dep)
                    bank_rd[db].append(tx)
                    x_last[(d, T)] = tx
                    if final_out and g == NFG - 1:
                        out_toks.append(B.dma("sp", out_v[:, d, tsl], xT[:, d, tsl], s_out, deps=[tx]))
                return last_pe

            units = [(g, T) for g in range(NFG) for T in range(NT)]
            nxt = emit_gu(*units[0])
            for i, (g, T) in enumerate(units):
                cur = nxt
                defer = early is not None and i + 1 < len(units) and units[i + 1] == (1, 0)
                if i + 1 < len(units) and not defer:
                    nxt = emit_gu(*units[i + 1])
                if i == 0 and early is not None:
                    early()
                last_pe = emit_down(g, T, cur)
                if epilogue is not None and g == NFG - 1:
                    epilogue(T, [x_last[(NCH - 1, T)]], [gu_pe[(g, T)]])
                if T == NT - 1:
                    w_release((name, g), [last_pe])
                    if ada_iter is not None:
                        ada_iter(g)
                if defer:
                    nxt = emit_gu(*units[i + 1])
            xd = [x_last[(NCH - 1, T)] for T in range(NT)]
            return xd, out_toks

        s_out = B.new_sem("out")

        U_OFF = P_OFF + 8 * S * 2
        preset()
        nb0 = make_norm(modT[:, 0:8], der[:, 0:8], 6, P_OFF, nrs=4)
        st0 = [nb0(T, [x_tok[T], t_const], [], [], phase="stats") for T in range(NT)]
        tk = None
        for i in range(4):
            tk = ada_slice(i)
        mod_finish0(0, tk)
        ada_next = [6]

        def early0():
            tk = ada_slice(4)
            tk = ada_slice(5)
            mod_finish0(1, tk)

        def ada_iter(g):
            if g == 0:
                return
            if ada_next[0] < 18:
                i = ada_next[0]
                ada_next[0] += 1
                tk = ada_slice(i)
                if i == 11:
                    mod_finish(1, tk)
                if i == 17:
                    mod_finish(2, tk)
            if g == NFG - 1:
                while ada_next[0] < 18:
                    ada_iter(-1)

        po[0] = P_OFF + 14336 + 4096
        h_tok0 = [nb0(T, [x_tok[T], t_const], [mod_tok[0]], [], phase="apply", info=st0[T]) for T in range(NT)]
        nb1 = make_norm(modT[:, 24:32], der[:, 16:24], 6, U_OFF, nsq=8)
        h_tok1 = [None] * NT
        pend1 = []

        def flush1():
            while pend1:
                T, sqs, xtoks, hrd = pend1.pop(0)
                info = nb1(T, xtoks, [], [], phase="st", info=sqs)
                h_tok1[T] = nb1(T, xtoks, [mod_tok[1]], hrd, phase="apply", info=info)

        def epi1(T, xtoks, hrd):
            flush1()
            pend1.append((T, nb1(T, xtoks, [], [], phase="sq"), xtoks, hrd))

        xd, _ = ffn_phase("ffn1", h_tok0, der[:, 8:16], ada_iter, False, epilogue=(None if stage == "ffn1" else epi1), early=early0)
        flush1()

        def finish(xdeps):
            toks = []
            for T in range(NT):
                for d in range(NCH):
                    toks.append(B.dma("sp", out_v[:, d, T * TB:(T + 1) * TB], xT[:, d, T * TB:(T + 1) * TB], s_out, deps=xdeps))
            B.wait_only("sp", [(s_out, B.dma_cnt[s_out])])

        if stage == "ffn1":
            bt = B.barrier()
            finish(bt)
            B.emit()
            return nc

        class _Stop(Exception):
            pass

        def stop_if(name):
            if stage == name:
                raise _Stop()

        bt = B.barrier()
        for e_ in ("pe", "act", "dve"):
            B.wait_only(e_, [t_eb, t_cl])
        preset()
        dbg_dump = []
        g1 = modT[:, 40:48]
        bufA = palloc(BF16, 8 * S).rearrange("p (c t) -> p c t", c=8)
        assert po[0] == U_OFF
        h_tok = [None] * NT
        if stage == 'mnorm':
            finish(bt); B.emit(); return nc
        po[0] = U_OFF
        qz = palloc(BF16, 2 * S).rearrange("p (m t) -> p m t", m=2)
        kT = palloc(BF16, S)
        vh = palloc(BF16, S).rearrange("p (t e) -> p t e", e=128)
        Eb = Rot([palloc(BF16, 512).rearrange("p (m q) -> p m q", m=2) for _ in range(4)])
        rSb = Rot([palloc(F32, 512).rearrange("p (m q) -> p m q", m=2) for _ in range(1)])
        odb = Rot([palloc(F32, 256) for _ in range(2)])
        sqb = Rot([palloc(BF16, 256) for _ in range(2)])
        rdb = Rot([palloc(F32, 256) for _ in range(2)])
        psqb = Rot([palloc(BF16, TB) for _ in range(2)])
        plrb = Rot([palloc(F32, TB) for _ in range(2)])
        bo_b = palloc(BF16, 128)
        assert po[0] <= ARENA, po[0]

        Sb, Ob, Sumb, Pb, PbC = Rot([0, 1, 6]), Rot([2, 3]), Rot([4, 5]), Rot([7, 4, 5, 2, 3]), Rot([7])
        t_qz = B.op("dve", lambda e: e.memset(qz[:], 0.0), deps=bt)
        t_bob = B.op("dve", lambda e: e.tensor_copy(bo_b[:], bo_f[:]), deps=bt)
        q_rd, k_rd, v_rd = list(bt) + [t_qz], list(bt), list(bt)
        gq = cA[:, A_GQ:A_GQ + 1]
        gk = cA[:, A_GK:A_GK + 1]

        for h in range(H):
            slot, lt = w_get(("att", h))
            wq = slot[:, 0:1024].rearrange("p (k n) -> p k n", k=8)
            wk = slot[:, 1024:2048].rearrange("p (k n) -> p k n", k=8)
            wv = slot[:, 2048:3072].rearrange("p (k n) -> p k n", k=8)
            units = [("q", T) for T in range(NT)] + [("k", T) for T in range(NT)]
            q_tok, k_tok = [None] * NT, [None] * NT
            pend = None

            def proj_mm(u):
                kind, T = u
                tsl = slice(T * TB, (T + 1) * TB)
                w = wq if kind == "q" else wk
                _, pb, _ = Pb.next()
                tp = mm_group(psum[:, pb, :], [(w[:, kc, :], hT[:, kc, tsl]) for kc in range(8)], [lt], pb)
                si, sq_ap, sdeps = psqb.next()
                ts = B.op("act", lambda e, pb=pb, sq_ap=sq_ap: e.activation(sq_ap[:], psum[:, pb, :], AF.Square), deps=[tp] + sdeps)
                psqb.wrote(si, ts)
                bank_rd[pb].append(ts)
                return (kind, T, tsl, pb, tp, si, sq_ap, ts)

            def proj_fin(st_):
                kind, T, tsl, pb, tp, si, sq_ap, ts = st_
                _, sbk, _ = Sb.next()
                tst = mm_group(psum[:, sbk, :], [(bo_b[:], sq_ap[:])], [ts, t_bob], sbk)
                psqb.read(si, tst)
                li, lr, ldeps = plrb.next()
                tl = B.op("act", lambda e, sbk=sbk, lr=lr: e.activation(lr[:], psum[:, sbk, :], AF.Ln, bias=EPS, scale=1.0 / 64),
                          deps=[tst] + ldeps)
                bank_rd[sbk].append(tl)
                tr = B.op("act", lambda e, lr=lr: e.activation(lr[:], lr[:], AF.Exp, scale=-0.5), deps=[tl])
                if kind == "q":
                    B.op("dve", lambda e, pb=pb, tsl=tsl, lr=lr: e.scalar_tensor_tensor(
                        qz[0:64, 0, tsl], psum[0:64, pb, :], gq[0:64, :], lr[0:64, :], ALU.mult, ALU.mult),
                        deps=[tr, tp, t_cl] + q_rd)
                    tq = B.op("dve", lambda e, pb=pb, tsl=tsl, lr=lr: e.scalar_tensor_tensor(
                        qz[64:128, 1, tsl], psum[64:128, pb, :], gq[64:128, :], lr[64:128, :], ALU.mult, ALU.mult),
                        deps=[tr, tp, t_cl] + q_rd)
                    q_tok[T] = tq
                else:
                    tq = B.op("dve", lambda e, pb=pb, tsl=tsl, lr=lr: e.scalar_tensor_tensor(
                        kT[:, tsl], psum[:, pb, :], gk, lr[:], ALU.mult, ALU.mult), deps=[tr, tp, t_cl] + k_rd)
                    k_tok[T] = tq
                bank_rd[pb].append(tq)
                plrb.wrote(li, tq)

            for u in units:
                st_ = proj_mm(u)
                if pend is not None:
                    proj_fin(pend)
                pend = st_
            v_tok = []
            last_v = None
            for tg in range(4):
                _, pb, _ = Pb.next()
                tv = None
                for tt in range(4):
                    t0 = (tg * 4 + tt) * 128
                    tv = mm_group(psum[:, pb, tt * 128:(tt + 1) * 128],
                                  [(hT[:, kc, t0:t0 + 128], wv[:, kc, :]) for kc in range(8)],
                                  [lt], pb, start=True)
                if pend is not None:
                    proj_fin(pend)
                    pend = None
                tc = B.op("dve", lambda e, pb=pb, tg=tg: e.tensor_copy(
                    vh[:, tg * 4:(tg + 1) * 4, :], psum[:, pb, :].rearrange("p (t e) -> p t e", e=128)),
                    deps=[tv] + v_rd)
                bank_rd[pb].append(tc)
                v_tok.append(tc)
                last_v = tv
            w_release(("att", h), [last_v])
            q_rd, k_rd, v_rd = [], [], []
            if stage == 'proj0':
                bt = B.barrier(); finish(bt); B.emit(); return nc

            pairs = [(Qb, j) for Qb in range(8) for j in range(2 * Qb + 2)]
            blk = {}

            def emit_S(pair):
                Qb, j = pair
                q0b = Qb * 256
                r = j - 2 * Qb
                qs = 128 if r == 1 else 0
                _, sbk, _ = Sb.next()
                Sv = psum[:, sbk, :].rearrange("p (m q) -> p m q", m=2)
                if qs == 0:
                    ts = B.op("pe", lambda e, Sv=Sv, j=j, q0b=q0b: e.matmul(
                        Sv[:, :, :], kT[:, j * 128:(j + 1) * 128], qz[:, :, q0b:q0b + 256], start=True, stop=True),
                        deps=[q_tok[q0b // TB], k_tok[(j * 128) // TB]] + bank_rd[sbk], signal=True)
                else:
                    for m in range(2):
                        ts = B.op("pe", lambda e, Sv=Sv, j=j, q0b=q0b, m=m: e.matmul(
                            Sv[:, m, 128:256], kT[:, j * 128:(j + 1) * 128], qz[:, m, q0b + 128:q0b + 256], start=True, stop=True),
                            deps=([q_tok[q0b // TB], k_tok[(j * 128) // TB]] + bank_rd[sbk]) if m == 0 else [], signal=(m == 1))
                bank_rd[sbk] = []
                ei, E, edeps = Eb.next()
                te = B.op("act", lambda e, E=E, Sv=Sv, qs=qs: e.activation(E[:, :, qs:256], Sv[:, :, qs:256], AF.Exp, scale=0.125),
                          deps=edeps + [ts])
                bank_rd[sbk].append(te)
                fix = []
                if r == -1:
                    fix = [(0, EBs)]
                elif r == 0:
                    fix = [(0, EBd), (128, EBs)]
                elif r == 1:
                    fix = [(128, EBd)]
                tl = te
                for (c0, EBt) in fix:
                    a = EBt[:, h, :]
                    ebb = bass.AP(a.tensor, a.offset, [list(a.ap[0]), [0, 2], list(a.ap[1])])
                    tl = B.op("dve", lambda e, E=E, c0=c0, ebb=ebb: e.tensor_tensor(
                        E[:, :, c0:c0 + 128], E[:, :, c0:c0 + 128], ebb, ALU.mult), deps=[te, t_eb])
                Eb.wrote(ei, tl)
                return (ei, E, qs, tl, te)

            def emit_PV(pair, einfo):
                Qb, j = pair
                ei, E, qs, tl, te = einfo
                nj = 2 * Qb + 2
                if j == 0:
                    _, ob, _ = Ob.next()
                    _, smb, _ = Sumb.next()
                    blk[Qb] = (ob, smb)
                ob, smb = blk[Qb]
                Ov = psum[:, ob, :].rearrange("p (m q) -> p m q", m=2)
                Smv = psum[:, smb, :].rearrange("p (m q) -> p m q", m=2)
                first, lastj = (j == 0), (j == nj - 1)
                if qs == 0:
                    B.op("pe", lambda e, Ov=Ov, E=E, j=j, first=first, lastj=lastj: e.matmul(
                        Ov[:, :, :], vh[:, j, :], E[:, :, :], start=first, stop=lastj),
                        deps=[tl, te, v_tok[j // 4]] + (bank_rd[ob] if first else []), signal=False)
                else:
                    for m in range(2):
                        B.op("pe", lambda e, Ov=Ov, E=E, j=j, m=m, first=first, lastj=lastj: e.matmul(
                            Ov[:, m, 128:256], vh[:, j, :], E[:, m, 128:256], start=first, stop=(lastj and m == 1)),
                            deps=([tl, te, v_tok[j // 4]] + (bank_rd[ob] if first else [])) if m == 0 else [], signal=False)
                if first:
                    bank_rd[ob] = []
                if qs == 0:
                    tpv = B.op("pe", lambda e, Smv=Smv, E=E, first=first, lastj=lastj: e.matmul(
                        Smv[:, :, :], ones_b[:], E[:, :, :], start=first, stop=lastj),
                        deps=(bank_rd[smb] if first else []), signal=True)
                else:
                    for m in range(2):
                        tpv = B.op("pe", lambda e, Smv=Smv, E=E, m=m, first=first, lastj=lastj: e.matmul(
                            Smv[:, m, 128:256], ones_b[:], E[:, m, 128:256], start=first, stop=(lastj and m == 1)),
                            deps=(bank_rd[smb] if (first and m == 0) else []), signal=(m == 1))
                if first:
                    bank_rd[smb] = []
                Eb.read(ei, tpv)
                return tpv

            def fin_A(Qb, tpv, stt):
                ob, smb = blk[Qb]
                Ov = psum[:, ob, :].rearrange("p (m q) -> p m q", m=2)
                Smv = psum[:, smb, :].rearrange("p (m q) -> p m q", m=2)
                ri, rS, rdeps = rSb.next()
                t1 = B.op("act", lambda e, rS=rS, Smv=Smv: e.activation(rS[:], Smv[:], AF.Ln), deps=rdeps + [tpv])
                bank_rd[smb].append(t1)
                t2 = B.op("act", lambda e, rS=rS: e.activation(rS[:], rS[:], AF.Exp, scale=-1.0), deps=[t1])
                t3 = B.op("dve", lambda e, rS=rS, Ov=Ov: e.tensor_tensor(rS[:], Ov[:], rS[:], ALU.mult), deps=[t2, tpv])
                bank_rd[ob].append(t3)
                oi, od, odeps = odb.next()
                t4 = B.op("dve", lambda e, od=od, rS=rS: e.scalar_tensor_tensor(od[:], rS[:, 1, :], neg_lam, rS[:, 0, :], ALU.mult, ALU.add),
                          deps=odeps + [t3, t_lam])
                rSb.wrote(ri, t4)
                stt.update(oi=oi, od=od, t4=t4)

            def fin_B(Qb, stt):
                od, t4 = stt["od"], stt["t4"]
                qi, sqv, qdeps = sqb.next()
                t5 = B.op("dve", lambda e, sqv=sqv, od=od: e.tensor_tensor(sqv[:], od[:], od[:], ALU.mult), deps=qdeps + [t4])
                _, pb, _ = PbC.next()
                t6 = mm_group(psum[:, pb, 0:256], [(ones_b[:], sqv[:])], [t5], pb)
                sqb.read(qi, t6)
                stt.update(pb=pb, t6=t6)

            def fin_C(Qb, stt):
                od, t4, pb, t6, oi = stt["od"], stt["t4"], stt["pb"], stt["t6"], stt["oi"]
                q0b = Qb * 256
                di, rd, ddeps = rdb.next()
                t7 = B.op("act", lambda e, rd=rd, pb=pb: e.activation(rd[:], psum[:, pb, 0:256], AF.Ln, bias=EPS, scale=1.0 / 128),
                          deps=ddeps + [t6])
                bank_rd[pb].append(t7)
                t8 = B.op("act", lambda e, rd=rd: e.activation(rd[:], rd[:], AF.Exp, scale=-0.5), deps=[t7])
                t9 = B.op("dve", lambda e, od=od, rd=rd, h=h, q0b=q0b: e.scalar_tensor_tensor(
                    bufA[:, h, q0b:q0b + 256], od[:], gsub8, rd[:], ALU.mult, ALU.mult), deps=[t8, t4, t_lam])
                odb.read(oi, t9)
                odb.wrote(oi, t4)
                rdb.wrote(di, t8)
                rdb.read(di, t9)

            sched = {}

            def at(i, fn):
                sched.setdefault(i, []).append(fn)

            PF = 2
            sq_ = [emit_S(pairs[k]) for k in range(PF)]
            last_tpv = None
            npairs = len(pairs)
            for i, pair in enumerate(pairs):
                if i + PF < npairs:
                    sq_.append(emit_S(pairs[i + PF]))
                cur = sq_.pop(0)
                tpv = emit_PV(pair, cur)
                last_tpv = tpv
                Qb, j = pair
                if j == 2 * Qb + 1:
                    stt = {}
                    e_next = i + 2 * (Qb + 1) + 2 if Qb < 7 else 10 ** 9
                    at(i + 1, lambda Qb=Qb, tpv=tpv, stt=stt: fin_A(Qb, tpv, stt))
                    at(min(i + 12, e_next), lambda Qb=Qb, stt=stt: fin_B(Qb, stt))
                    at(min(i + 16, e_next + 1), lambda Qb=Qb, stt=stt: fin_C(Qb, stt))
                for fn in sched.pop(i, []):
                    fn()
            for i in sorted(sched):
                for fn in sched[i]:
                    fn()
            q_rd.append(last_tpv)
            k_rd.append(last_tpv)
            v_rd.append(last_tpv)
            if stage == 'head0':
                bt = B.barrier(); finish(bt); B.emit(); return nc

        def branch_out(pname, wname, src, mbuf, sgbuf, bt, epilogue=None):
            rots = (Rot([0, 1]), Rot([2, 3]), Rot([4, 5, 6, 7]))
            m_tok = {}
            sg_rd = list(bt)
            for i in range(4):
                slot, lt = w_get((pname, i))
                wp = slot[:, 0:2048].rearrange("p (k n) -> p k n", k=8)
                wgt = slot[:, 2048:4096].rearrange("p (k n) -> p k n", k=8)
                last = None
                for T in range(NT):
                    tsl = slice(T * TB, (T + 1) * TB)
                    for dd in range(2):
                        d = 2 * i + dd
                        _, yb, _ = rots[0].next()
                        _, gb, _ = rots[1].next()
                        ty = mm_group(psum[:, yb, :], [(wp[:, kc, dd * 128:(dd + 1) * 128], src[:, kc, tsl]) for kc in range(8)],
                                      [lt] + bt, yb)
                        tg = mm_group(psum[:, gb, :], [(wgt[:, kc, dd * 128:(dd + 1) * 128], hT[:, kc, tsl]) for kc in range(8)],
                                      [lt], gb)
                        last = tg
                        ta = B.op("act", lambda e, gb=gb: e.activation(sgbuf[:], psum[:, gb, :], AF.Sigmoid), deps=[tg] + sg_rd)
                        bank_rd[gb].append(ta)
                        tm = B.op("dve", lambda e, d=d, yb=yb, tsl=tsl: e.tensor_tensor(mbuf[:, d, tsl], sgbuf[:], psum[:, yb, :], ALU.mult),
                                  deps=[ta, ty] + bt)
                        bank_rd[yb].append(tm)
                        sg_rd = [tm]
                        m_tok[(d, T)] = tm
                w_release((pname, i), [last])
            last_proj = last
            for i in range(2):
                slot, lt = w_get((wname, i))
                wo = slot[:, 0:4096].rearrange("p (k n) -> p k n", k=8)
                last = None
                for T in range(NT):
                    tsl = slice(T * TB, (T + 1) * TB)
                    for dd in range(4):
                        d = 4 * i + dd
                        _, ob, _ = rots[2].next()
                        to = mm_group(psum[:, ob, :], [(wo[:, kc, dd * 128:(dd + 1) * 128], mbuf[:, kc, tsl]) for kc in range(8)],
                                      [lt, m_tok[(7, T)]], ob)
                        last = to
                        tx = B.op("dve", lambda e, d=d, ob=ob, tsl=tsl: e.scalar_tensor_tensor(
                            xT[:, d, tsl], psum[:, ob, :], g1[:, d:d + 1], xT[:, d, tsl], ALU.mult, ALU.add),
                            deps=[to, mod_tok[1]] + bt)
                        bank_rd[ob].append(tx)
                    if epilogue is not None and i == 1:
                        epilogue(T, [tx], [last_proj])
                w_release((wname, i), [last])

        bt = B.barrier()
        po[0] = U_OFF
        mbuf = palloc(BF16, 8 * S).rearrange("p (c t) -> p c t", c=8)
        sgbuf = palloc(F32, TB)
        assert po[0] <= ARENA, po[0]
        branch_out("aproj", "woa", bufA, mbuf, sgbuf, bt)

        if stage == "attn":
            bt = B.barrier()
            finish(bt)
            B.emit()
            return nc

        bt = B.barrier()
        po[0] = U_OFF
        cBs = palloc(F32, NB)
        wTm = palloc(BF16, 1024).rearrange("p (g t) -> p g t", g=8)
        gfb = Rot([palloc(F32, 1024) for _ in range(2)])
        vnb = Rot([palloc(BF16, 1024) for _ in range(2)])
        tmpb = Rot([palloc(F32, TB) for _ in range(2)])
        stbb = Rot([palloc(F32, 16) for _ in range(2)])
        assert po[0] <= ARENA, po[0]
        gu = bufA
        s_cb = B.new_sem("cB")
        t_cB = B.dma("sp", cBs[:], cB_d, s_cb, deps=bt)
        tri = cBs[:, B_TRI:B_TRI + 128]
        wst = cBs[:, B_WST:B_WST + 1024].rearrange("p (g t) -> p g t", g=8)
        lng_c = cBs[:, B_LNG:B_LNG + 8]
        lnb_c = cBs[:, B_LNB:B_LNB + 8]
        bs_bc = cBs[:, B_BS:B_BS + 1024]
        t_w = None
        for g in range(8):
            t_w = B.op("dve", lambda e, g=g: e.tensor_tensor(wTm[:, g, :], wst[:, g, :], tri, ALU.mult), deps=[t_cB] + bt)
        t_bg = None
        for half in range(2):
            trs = mm_group(psum[:, 6 + half, :], [(ones_b[:], wTm[:, 4 * half:4 * half + 4, :])], [t_w] + bt, 6 + half)
            for gg in range(4):
                g = 4 * half + gg
                t_bg = B.op("dve", lambda e, g=g, gg=gg, half=half: e.scalar_tensor_tensor(
                    bs_bc[:, g * 128:(g + 1) * 128], psum[:, 6 + half, gg * 128:(gg + 1) * 128], lnb_c[:, g:g + 1],
                    bs_bc[:, g * 128:(g + 1) * 128], ALU.mult, ALU.add), deps=[trs, t_cB])
            bank_rd[6 + half].append(t_bg)
        u_tok = {}
        Ub = Rot([0, 1, 2, 3])
        for i in range(2):
            slot, lt = w_get(("u", i))
            w = slot[:, 0:4096].rearrange("p (k n) -> p k n", k=8)
            last = None
            for T in range(NT):
                tsl = slice(T * TB, (T + 1) * TB)
                for cc in range(4):
                    c = 4 * i + cc
                    _, ub, _ = Ub.next()
                    tu = mm_group(psum[:, ub, :], [(w[:, kc, cc * 128:(cc + 1) * 128], hT[:, kc, tsl]) for kc in range(8)],
                                  [lt] + bt, ub)
                    last = tu
                    ta = B.op("act", lambda e, c=c, ub=ub, tsl=tsl: e.activation(gu[:, c, tsl], psum[:, ub, :], AF.Gelu), deps=[tu] + bt)
                    bank_rd[ub].append(ta)
                    u_tok[(c, T)] = ta
            w_release(("u", i), [last])
        slot0, lt0 = w_get(("gv", 0))
        slot1, lt1 = w_get(("gv", 1))
        wgv = [slot0[:, 0:4096].rearrange("p (k n) -> p k n", k=8), slot1[:, 0:4096].rearrange("p (k n) -> p k n", k=8)]
        Gb, Fb = Rot([0, 1, 2, 3]), Rot([4, 5, 6, 7])
        last_gv = None

        def gv_A(tt):
            nonlocal last_gv
            t0 = tt * 128
            gi, gfa, gdeps = gfb.next()
            si, sta, sdeps = stbb.next()
            tstat = []
            for half in range(2):
                _, gb, _ = Gb.next()
                tg = mm_group(psum[:, gb, :], [(hT[:, kc, t0:t0 + 128], wgv[half][:, kc, :]) for kc in range(8)], [lt0, lt1] + bt, gb)
                last_gv = tg
                ta = B.op("act", lambda e, half=half, gb=gb, gfa=gfa: e.activation(gfa[:, half * 512:(half + 1) * 512], psum[:, gb, :], AF.Gelu),
                          deps=[tg] + gdeps)
                bank_rd[gb].append(ta)
                tstat.append(B.op("dve", lambda e, half=half, gfa=gfa, sta=sta: e.bn_stats(sta[:, half * 6:(half + 1) * 6], gfa[:, half * 512:(half + 1) * 512]),
                                  deps=[ta] + sdeps))
            t = B.op("dve", lambda e, sta=sta: e.bn_aggr(sta[:, 12:14], sta[:, 0:12]), deps=tstat)
            t = B.op("act", lambda e, sta=sta: e.activation(sta[:, 14:15], sta[:, 13:14], AF.Ln, bias=EPS), deps=[t])
            t = B.op("act", lambda e, sta=sta: e.activation(sta[:, 14:15], sta[:, 14:15], AF.Exp, scale=-0.5), deps=[t])
            t = B.op("dve", lambda e, sta=sta: e.scalar_tensor_tensor(sta[:, 15:16], sta[:, 12:13], -1.0, sta[:, 14:15], ALU.mult, ALU.mult), deps=[t])
            vi, vn, vdeps = vnb.next()
            tvn = B.op("act", lambda e, gfa=gfa, sta=sta, vn=vn: e.activation(vn[:], gfa[:], AF.Identity, bias=sta[:, 15:16], scale=sta[:, 14:15]),
                       deps=[t] + vdeps)
            stbb.wrote(si, tvn)
            gfb.wrote(gi, tvn)
            vnb.wrote(vi, tvn)
            return (vi, vn, tvn)

        def gv_B(tt, info):
            vi, vn, tvn = info
            t0 = tt * 128
            for fi in range(2):
                _, fb, _ = Fb.next()
                tf = None
                for gg in range(4):
                    g = fi * 4 + gg
                    tf = mm_group(psum[:, fb, gg * 128:(gg + 1) * 128], [(vn[:, g * 128:(g + 1) * 128], wTm[:, g, :])],
                                  [tvn, t_w], fb, start=True)
                vnb.read(vi, tf)
                pi, tp, pdeps = tmpb.next()
                t1 = None
                for gg in range(4):
                    g = fi * 4 + gg
                    t1 = B.op("dve", lambda e, tp=tp, fb=fb, g=g, gg=gg: e.scalar_tensor_tensor(
                        tp[:, gg * 128:(gg + 1) * 128], psum[:, fb, gg * 128:(gg + 1) * 128], lng_c[:, g:g + 1],
                        bs_bc[:, g * 128:(g + 1) * 128], ALU.mult, ALU.add), deps=[tf, t_cB, t_bg] + pdeps)
                bank_rd[fb].append(t1)
                t2 = B.op("dve", lambda e, tp=tp, fi=fi, t0=t0: e.tensor_tensor(
                    gu[:, 4 * fi:4 * fi + 4, t0:t0 + 128], gu[:, 4 * fi:4 * fi + 4, t0:t0 + 128],
                    tp[:].rearrange("p (g t) -> p g t", g=4), ALU.mult),
                    deps=[t1] + [u_tok[(4 * fi + gg, tt // 4)] for gg in range(4)])
                tmpb.wrote(pi, t2)

        nxt = gv_A(0)
        for tt in range(16):
            cur = nxt
            if tt + 1 < 16:
                nxt = gv_A(tt + 1)
            gv_B(tt, cur)
        w_release(("gv", 0), [last_gv])
        w_release(("gv", 1), [last_gv])

        bt = B.barrier()
        po[0] = U_OFF
        mbuf = palloc(BF16, 8 * S).rearrange("p (c t) -> p c t", c=8)
        sgbuf = palloc(F32, TB)
        nb2 = make_norm(modT[:, 48:56], der[:, 24:32], 7, P_OFF, nsq=8)
        h_tok2 = [None] * NT
        pend2 = []

        def flush2():
            while pend2:
                T, sqs, xtoks, hrd = pend2.pop(0)
                info = nb2(T, xtoks + hrd, [], [], phase="st", info=sqs)
                h_tok2[T] = nb2(T, xtoks + hrd, [mod_tok[2]], hrd, phase="apply", info=info)

        def epi2(T, xtoks, hrd):
            flush2()
            pend2.append((T, nb2(T, xtoks + hrd, [], [], phase="sq"), xtoks, hrd))

        branch_out("bproj", "wob", gu, mbuf, sgbuf, bt, epilogue=(None if stage == "mixer" else epi2))
        flush2()

        if stage == "mixer":
            bt = B.barrier()
            finish(bt)
            B.emit()
            return nc

        bt = B.barrier()
        po[0] = P_OFF + 20480
        _, out_toks = ffn_phase("ffn2", h_tok2, der[:, 32:40], None, True)
        B.wait_only("sp", [(s_out, B.dma_cnt[s_out])])
        B.emit()
    return nc


_CACHE = {}


def kernel(**inp):
    x = np.asarray(inp["x"], np.float32)
    c = np.asarray(inp["c"], np.float32)
    inp = {k: np.asarray(v) for k, v in inp.items()}
    wblob = build_wblob(inp)
    cA, cB, cC = build_consts(inp)
    if "nc" not in _CACHE:
        _CACHE["nc"] = build_program("full")
    nc = _CACHE["nc"]
    in_maps = []
    for b in range(8):
        in_maps.append({"xT": np.ascontiguousarray(x[b].T), "cv": np.ascontiguousarray(c[b].reshape(8, 128).T),
                        "cA": cA, "cB": cB, "cC": cC, "wblob": wblob})
    res = run_bass_kernel_spmd(nc, in_maps, core_ids=list(range(8)))
    out = np.stack([np.asarray(res.results[b]["outT"]).T for b in range(8)])
    return np.ascontiguousarray(out.astype(np.float32))
```

```python
import math
import numpy as np
from contextlib import ExitStack
import concourse.bass as bass
import concourse.mybir as mybir
from concourse.bass_utils import run_bass_kernel_spmd

F32 = mybir.dt.float32
BF16 = mybir.dt.bfloat16
U8 = mybir.dt.uint8
AF = mybir.ActivationFunctionType
ALU = mybir.AluOpType

D = 1024
S = 2048
DFF = 2816
NCH = 8
NT = 4
TB = 512
H = 8
EPS = 1e-6
COL_Q, COL_K, COL_V, COL_U, COL_GV, COL_GATE = 0, 1024, 2048, 3072, 4096, 5120
SLOT = 6144
NSLOT = 3
NFG = 11
LAM_INIT = 0.8 - 0.6 * math.exp(-0.3 * 0)
SEM_LIMIT = 30000

A_BADA = 0
A_GQ = 72
A_GK = 73
A_GSUB = 74
A_LAM = 75
NA = A_LAM + 256
B_TRI = 0
B_WST = 128
B_LNG = B_WST + 1024
B_LNB = B_LNG + 8
B_BS = B_LNB + 8
NB = B_BS + 1024


class Builder:
    ENGS = ("pe", "act", "dve", "pool", "sp")

    def __init__(self, nc, stack):
        self.nc = nc
        self.stack = stack
        self.ops = {e: [] for e in self.ENGS}
        self.sems = []
        self.cur = {}
        self.waited = {e: {} for e in self.ENGS}
        for e in self.ENGS:
            self.cur[e] = [self.new_sem("prog_" + e), 0]
        self.dma_cnt = {}

    def new_sem(self, name):
        h = self.stack.enter_context(self.nc.semaphore(name + "_%d" % len(self.sems)))
        self.sems.append(h)
        return len(self.sems) - 1

    def _waits(self, eng, deps):
        waits = []
        for t in deps:
            if t is None:
                continue
            s, v = t
            if self.waited[eng].get(s, 0) >= v:
                continue
            self.waited[eng][s] = v
            waits.append((s, v))
        return waits

    def op(self, eng, fn, deps=(), signal=True):
        waits = self._waits(eng, deps)
        inc = None
        tok = None
        if signal:
            cur = self.cur[eng]
            if cur[1] >= SEM_LIMIT:
                cur[0] = self.new_sem("prog_" + eng)
                cur[1] = 0
            cur[1] += 1
            inc = (cur[0], 1)
            tok = (cur[0], cur[1])
        self.ops[eng].append((waits, fn, inc))
        return tok

    def dma(self, eng, out, in_, sem, deps=()):
        waits = self._waits(eng, deps)
        self.dma_cnt[sem] = self.dma_cnt.get(sem, 0) + 16
        self.ops[eng].append((waits, lambda e: e.dma_start(out=out, in_=in_), (sem, 16)))
        return (sem, self.dma_cnt[sem])

    def wait_only(self, eng, deps):
        waits = self._waits(eng, deps)
        if waits:
            self.ops[eng].append((waits, None, None))

    def last(self, eng):
        c = self.cur[eng]
        return (c[0], c[1]) if c[1] > 0 else None

    def barrier(self):
        toks = {e: self.last(e) for e in ("pe", "act", "dve")}
        for e in ("pe", "act", "dve"):
            self.wait_only(e, [toks[o] for o in ("pe", "act", "dve") if o != e])
        return [t for t in toks.values() if t is not None]

    def emit(self):
        nc = self.nc
        with nc.Block() as block:
            def body(name):
                def f(e):
                    for waits, fn, inc in self.ops[name]:
                        for s, v in waits:
                            e.wait_ge(self.sems[s], v)
                        if fn is None:
                            continue
                        ins = fn(e)
                        if inc is not None:
                            ins.then_inc(self.sems[inc[0]], inc[1])
                return f
            block.tensor(body("pe"))
            block.scalar(body("act"))
            block.vector(body("dve"))
            block.gpsimd(body("pool"))
            block.sync(body("sp"))


class Rot:
    def __init__(self, aps):
        self.aps = aps
        self.i = -1
        self.readers = [[] for _ in aps]
        self.writer = [None for _ in aps]

    def next(self):
        self.i = (self.i + 1) % len(self.aps)
        deps = list(self.readers[self.i])
        if self.writer[self.i] is not None:
            deps.append(self.writer[self.i])
        self.readers[self.i] = []
        self.writer[self.i] = None
        return self.i, self.aps[self.i], deps

    def wrote(self, i, tok):
        self.writer[i] = tok

    def read(self, i, tok):
        self.readers[i].append(tok)


def _pk(W, c0, c1):
    n = c1 - c0
    return np.ascontiguousarray(W[:, c0:c1].reshape(8, 128, n).transpose(1, 0, 2).reshape(128, 8 * n))


def slice_plan():
    order = []
    for i in range(4):
        order.append(("ada", i))
    order.append(("ffn1", 0))
    order.append(("ada", 4))
    order.append(("ada", 5))
    nxt = 6
    for g in range(1, NFG):
        order.append(("ffn1", g))
        if nxt < 18:
            order.append(("ada", nxt))
            nxt += 1
    while nxt < 18:
        order.append(("ada", nxt))
        nxt += 1
    for h in range(H):
        order.append(("att", h))
    for i in range(4):
        order.append(("aproj", i))
    for i in range(2):
        order.append(("woa", i))
    for i in range(2):
        order.append(("u", i))
    for i in range(2):
        order.append(("gv", i))
    for i in range(4):
        order.append(("bproj", i))
    for i in range(2):
        order.append(("wob", i))
    for g in range(NFG):
        order.append(("ffn2", g))
    return order


def slice_size(kind):
    return {"ada": 4096, "ffn1": 6144, "ffn2": 6144, "att": 3072, "aproj": 4096, "woa": 4096,
            "u": 4096, "gv": 4096, "bproj": 4096, "wob": 4096}[kind]


def build_wblob(inp):
    w_ada = inp["w_ada"][0]
    w_in = inp["w_in"][0]
    parts = []
    for kind, i in slice_plan():
        if kind == "ada":
            a = _pk(w_ada, 512 * i, 512 * i + 512)
        elif kind in ("ffn1", "ffn2"):
            if kind == "ffn1":
                Wg, Wu, Wd = inp["w_ffn1_gate"][0], inp["w_ffn1_up"][0], inp["w_ffn1_down"][0]
            else:
                Wg, Wu, Wd = inp["w_ffn2_gate"][0], inp["w_ffn2_up"][0], inp["w_ffn2_down"][0]
            wd = Wd[256 * i:256 * i + 256, :].reshape(2, 128, 1024).transpose(1, 0, 2).reshape(128, 2048)
            a = np.concatenate([_pk(Wg, 256 * i, 256 * i + 256), _pk(Wu, 256 * i, 256 * i + 256), wd], axis=1)
        elif kind == "att":
            a = np.concatenate([_pk(w_in, COL_Q + 128 * i, COL_Q + 128 * i + 128),
                                _pk(w_in, COL_K + 128 * i, COL_K + 128 * i + 128),
                                _pk(w_in, COL_V + 128 * i, COL_V + 128 * i + 128)], axis=1)
        elif kind == "aproj":
            a = np.concatenate([_pk(inp["w_a_proj"][0], 256 * i, 256 * i + 256),
                                _pk(w_in, COL_GATE + 256 * i, COL_GATE + 256 * i + 256)], axis=1)
        elif kind == "bproj":
            a = np.concatenate([_pk(inp["w_b_proj"][0], 256 * i, 256 * i + 256),
                                _pk(w_in, COL_GATE + 1024 + 256 * i, COL_GATE + 1024 + 256 * i + 256)], axis=1)
        elif kind in ("woa", "wob"):
            a = _pk(inp["w_o"][0], 512 * i, 512 * i + 512)
        elif kind == "u":
            a = _pk(w_in, COL_U + 512 * i, COL_U + 512 * i + 512)
        elif kind == "gv":
            a = _pk(w_in, COL_GV + 512 * i, COL_GV + 512 * i + 512)
        assert a.shape == (128, slice_size(kind)), (kind, a.shape)
        parts.append(a.astype(np.float32, copy=False))
    return np.ascontiguousarray(np.concatenate(parts, axis=1))


def rel_bucket_np(n):
    n = np.asarray(n)
    max_exact = 16
    nf = np.maximum(n, 1).astype(np.float32)
    large = max_exact + (np.log(nf / np.float32(max_exact)) / np.float32(math.log(128 / max_exact))
                         * np.float32(32 - max_exact)).astype(np.int32)
    large = np.minimum(large, 31)
    return np.where(n < max_exact, n, large)


def build_consts(inp):
    cA = np.zeros((128, NA), np.float32)
    cA[:, A_BADA:A_BADA + 72] = inp["b_ada"][0].reshape(72, 128).T
    cA[:, A_GQ] = np.tile(inp["q_norm_g"][0], 2)
    cA[:, A_GK] = np.tile(inp["k_norm_g"][0], 2)
    cA[:, A_GSUB] = inp["subln_g"][0]
    lam = np.concatenate([inp["lam_q1"][0], inp["lam_k1"][0], inp["lam_q2"][0], inp["lam_k2"][0]])
    cA[:, A_LAM:A_LAM + 256] = lam[None, :]
    cB = np.zeros((128, NB), np.float32)
    s = np.arange(128)
    cB[:, B_TRI:B_TRI + 128] = (s[:, None] <= s[None, :]).astype(np.float32)
    cB[:, B_WST:B_WST + 1024] = inp["w_spatial"][0].transpose(2, 0, 1).reshape(128, 1024)
    cB[:, B_LNG:B_LNG + 8] = inp["gmlp_ln_g"][0].reshape(8, 128).T
    cB[:, B_LNB:B_LNB + 8] = inp["gmlp_ln_b"][0].reshape(8, 128).T
    cB[:, B_BS:B_BS + 1024] = inp["b_spatial"][0].reshape(1, 1024)
    cC = np.zeros((32, 8 + 256), np.float32)
    cC[:, 0:8] = inp["rel_bias_table"]
    bk = rel_bucket_np(np.arange(256))
    oh = np.zeros((32, 256), np.float32)
    oh[bk, np.arange(256)] = 1.0
    oh[31, :] -= 1.0
    cC[:, 8:] = oh
    return cA, cB, cC


def build_program(stage="full"):
    nc = bass.Bass("TRN2", target_bir_lowering=False)
    plan = slice_plan()
    offs = []
    o = 0
    for kind, i in plan:
        offs.append(o)
        o += slice_size(kind)
    WTOT = o
    sidx = {k: j for j, k in enumerate(plan)}

    xT_d = nc.dram_tensor("xT", [D, S], F32, kind="ExternalInput").ap()
    cv_d = nc.dram_tensor("cv", [128, 8], F32, kind="ExternalInput").ap()
    cA_d = nc.dram_tensor("cA", [128, NA], F32, kind="ExternalInput").ap()
    cB_d = nc.dram_tensor("cB", [128, NB], F32, kind="ExternalInput").ap()
    cC_d = nc.dram_tensor("cC", [32, 264], F32, kind="ExternalInput").ap()
    wb_d = nc.dram_tensor("wblob", [128, WTOT], F32, kind="ExternalInput").ap()
    out_d = nc.dram_tensor("outT", [D, S], F32, kind="ExternalOutput").ap()
    scr_h = nc.dram_tensor("ebscr", [8, 384], BF16, kind="Internal")
    xT_v = xT_d.rearrange("(c p) t -> p c t", p=128)
    out_v = out_d.rearrange("(c p) t -> p c t", p=128)

    with ExitStack() as st:
        B = Builder(nc, st)
        ARENA = 212000
        arena = st.enter_context(nc.sbuf_tensor("arena", [128, ARENA], U8))
        psum = st.enter_context(nc.psum_tensor("psum", [128, 8, 512], F32))

        def view(off, dtype, n):
            nb = n * (4 if dtype == F32 else 2)
            assert off % 4 == 0 and off + nb <= ARENA, (off, nb)
            return arena[:, off:off + nb].bitcast(dtype)

        X_OFF, H_OFF, W_OFF, C_OFF, P_OFF = 0, 65536, 98304, 98304 + NSLOT * SLOT * 2, 98304 + NSLOT * SLOT * 2 + 8192
        xT = view(X_OFF, F32, 8 * S).rearrange("p (c t) -> p c t", c=8)
        hT = view(H_OFF, BF16, 8 * S).rearrange("p (c t) -> p c t", c=8)
        slots = [view(W_OFF + i * SLOT * 2, BF16, SLOT) for i in range(NSLOT)]
        co = [C_OFF]

        def calloc(dtype, n):
            v = view(co[0], dtype, n)
            co[0] += ((n * (4 if dtype == F32 else 2) + 3) // 4) * 4
            assert co[0] <= P_OFF
            return v
        cA = calloc(F32, NA)
        modT = calloc(F32, 72)
        modraw = calloc(F32, 72)
        der = calloc(F32, 48)
        cvs = calloc(F32, 8)
        sc_bf = calloc(BF16, 8)
        ones_f = calloc(F32, 128)
        bo_f = calloc(F32, 128)
        ones_b = calloc(BF16, 128)
        EB2 = calloc(BF16, 2048).rearrange("p (h q) -> p h q", h=8)
        EBd = EB2[:, :, 0:128]
        EBs = EB2[:, :, 128:256]
        lamw = calloc(F32, 16)
        so = ARENA - 4096
        cC = view(so, F32, 264)
        ebrow = view(so + 1056, BF16, 384)
        lamtmp = view(so + 1056 + 768, F32, 128)

        po = [P_OFF]

        def preset():
            po[0] = P_OFF

        def palloc(dtype, n):
            v = view(po[0], dtype, n)
            po[0] += ((n * (4 if dtype == F32 else 2) + 3) // 4) * 4
            return v

        wsem = [B.new_sem("wslot") for _ in range(NSLOT)]
        W = {"next": 0, "load": {}, "rel": {}}

        def w_ensure(i):
            while W["next"] <= i and W["next"] < len(plan):
                j = W["next"]
                assert j < NSLOT or (j - NSLOT) in W["rel"], (j, plan[j])
                deps = W["rel"].get(j - NSLOT, []) if j >= NSLOT else []
                if j == 0:
                    deps = list(deps) + [x_tok[0], x_tok[1], x_tok[2]]
                n = slice_size(plan[j][0])
                W["load"][j] = B.dma("pool", slots[j % NSLOT][:, 0:n], wb_d[:, offs[j]:offs[j] + n],
                                     wsem[j % NSLOT], deps)
                W["next"] += 1

        def w_get(key):
            i = sidx[key]
            w_ensure(i)
            for j in range(i + 1, i + NSLOT):
                if j - NSLOT < 0 or (j - NSLOT) in W["rel"]:
                    w_ensure(j)
                else:
                    break
            return slots[i % NSLOT], W["load"][i]

        def w_release(key, toks):
            W["rel"][sidx[key]] = list(toks)

        bank_rd = [[] for _ in range(8)]

        def mm_group(out_ap, pairs, deps, bank, start=True, stop=True, sig=True):
            tok = None
            n = len(pairs)
            for i, (l, r) in enumerate(pairs):
                d = list(deps) + bank_rd[bank] if i == 0 else ()
                if i == 0 and start:
                    bank_rd[bank] = []
                tok = B.op("pe", lambda e, l=l, r=r, s0=(start and i == 0), s1=(stop and i == n - 1):
                           e.matmul(out_ap, l, r, start=s0, stop=s1),
                           deps=d, signal=(sig and i == n - 1))
            return tok

        s_c = B.new_sem("cload")
        t_cA = B.dma("sp", cA[:], cA_d, s_c)
        t_cv = B.dma("sp", cvs[:], cv_d, s_c)
        t_cC = B.dma("sp", cC[0:32, :], cC_d, s_c)
        t_cl = (s_c, 48)
        xsem = [B.new_sem("xload") for _ in range(NT)]
        x_tok = [B.dma("sp", xT[:, :, T * TB:(T + 1) * TB], xT_v[:, :, T * TB:(T + 1) * TB], xsem[T]) for T in range(NT)]

        t = B.op("dve", lambda e: e.memset(ones_f[:], 1.0))
        t = B.op("dve", lambda e: e.memset(bo_f[:], 0.0))
        t = B.op("dve", lambda e: e.memset(bo_f[0:64, 0:64], 1.0), deps=[t])
        t = B.op("dve", lambda e: e.memset(bo_f[64:128, 64:128], 1.0), deps=[t])
        t = B.op("dve", lambda e: e.memset(ones_b[:], 1.0))
        t_ebz = B.op("dve", lambda e: e.memset(ebrow[0:8, :], 0.0))
        t_const = t_ebz
        t_sc = B.op("act", lambda e: e.activation(sc_bf[:], cvs[:], AF.Silu), deps=[t_cl])

        lv = cA[:, A_LAM:A_LAM + 256]
        t1 = B.op("dve", lambda e: e.tensor_tensor(lamtmp[:, 0:64], lv[:, 0:64], lv[:, 64:128], ALU.mult), deps=[t_cl])
        t2 = B.op("dve", lambda e: e.tensor_tensor(lamtmp[:, 64:128], lv[:, 128:192], lv[:, 192:256], ALU.mult), deps=[t_cl])
        t1 = B.op("dve", lambda e: e.reduce_sum(lamw[:, 0:1], lamtmp[:, 0:64], mybir.AxisListType.X), deps=[t1])
        t2 = B.op("dve", lambda e: e.reduce_sum(lamw[:, 1:2], lamtmp[:, 64:128], mybir.AxisListType.X), deps=[t2])
        t3 = B.op("act", lambda e: e.activation(lamw[:, 2:4], lamw[:, 0:2], AF.Exp), deps=[t1, t2])
        t4 = B.op("dve", lambda e: e.tensor_tensor(lamw[:, 4:5], lamw[:, 3:4], lamw[:, 2:3], ALU.subtract), deps=[t3])
        t4 = B.op("dve", lambda e: e.tensor_scalar(lamw[:, 4:5], lamw[:, 4:5], -LAM_INIT, None, ALU.add), deps=[t4])
        t5 = B.op("dve", lambda e: e.tensor_scalar(lamw[:, 5:6], cA[:, A_GSUB:A_GSUB + 1], 1.0 - LAM_INIT, None, ALU.mult), deps=[t_cl])
        t_lam = t5
        neg_lam = lamw[:, 4:5]
        gsub8 = lamw[:, 5:6]

        tb = mm_group(psum[0:8, 7, 0:256], [(cC[0:32, 0:8], cC[0:32, 8:264])], [t_cl], 7)
        te = B.op("act", lambda e: e.activation(ebrow[0:8, 127:383], psum[0:8, 7, 0:256], AF.Exp), deps=[tb, t_ebz])
        bank_rd[7].append(te)
        s_eb = B.new_sem("eb")
        tw = B.dma("sp", scr_h.ap(), ebrow[0:8, :], s_eb, deps=[te])
        for k in range(128):
            B.dma("sp", EB2[k:k + 1, :, :], bass.AP(scr_h, 127 - k, [[0, 1], [384, 8], [1, 256]]), s_eb, deps=[tw])
        t_eb = (s_eb, B.dma_cnt[s_eb])

        modps = psum[:, 7, 256:328]
        mod_tok = {}

        def ada_slice(i):
            slot, lt = w_get(("ada", i))
            w = slot[:, 0:4096].rearrange("p (k n) -> p k n", k=8)
            tok = None
            for sub in range(4):
                jc = 4 * i + sub
                tok = mm_group(modps[:, jc:jc + 1],
                               [(w[:, kc, sub * 128:(sub + 1) * 128], sc_bf[:, kc:kc + 1]) for kc in range(8)],
                               [lt, t_sc], 7)
            w_release(("ada", i), [tok])
            a0 = 4 * i
            tcp = B.op("dve", lambda e, a0=a0: e.tensor_copy(modraw[:, a0:a0 + 4], modps[:, a0:a0 + 4]), deps=[tok])
            bank_rd[7].append(tcp)
            return tcp

        hg_dep = []

        def mod_finish0(part, tok):
            a, b = (0, 16) if part == 0 else (16, 24)
            t = B.op("dve", lambda e: e.tensor_tensor(modT[:, a:b], modraw[:, a:b], cA[:, A_BADA + a:A_BADA + b], ALU.add),
                     deps=[tok, t_cl])
            if part == 0:
                t2 = B.op("dve", lambda e: e.tensor_scalar(der[:, 0:8], modT[:, 8:16], 1.0, None, ALU.add), deps=[t])
                mod_tok[0] = t2
            else:
                t2 = B.op("dve", lambda e: e.tensor_scalar(der[:, 8:16], modT[:, 16:24], 0.5, None, ALU.mult), deps=[t])
                hg_dep.append(t2)
            return t2

        def mod_finish(k, tok):
            a, b = 24 * k, 24 * k + 24
            t = B.op("dve", lambda e: e.tensor_tensor(modT[:, a:b], modraw[:, a:b], cA[:, A_BADA + a:A_BADA + b], ALU.add),
                     deps=[tok, t_cl])
            dcol = {0: 0, 1: 16, 2: 24}[k]
            t2 = B.op("dve", lambda e: e.tensor_scalar(der[:, dcol:dcol + 8], modT[:, a + 8:a + 16], 1.0, None, ALU.add), deps=[t])
            if k != 1:
                hcol = 8 if k == 0 else 32
                t2 = B.op("dve", lambda e: e.tensor_scalar(der[:, hcol:hcol + 8], modT[:, a + 16:a + 24], 0.5, None, ALU.mult), deps=[t])
            mod_tok[k] = t2
            return t2

        def make_norm(sh, sc1p, statbank, off, nrs=2, nsq=2):
            save = po[0]
            po[0] = off
            sq = Rot([palloc(BF16, TB) for _ in range(nsq)])
            lb = palloc(F32, TB)
            tmp = Rot([palloc(F32, TB) for _ in range(2)])
            rs = Rot([palloc(F32, TB) for _ in range(nrs)])
            po[0] = save
            stt = {"lb_rd": []}

            def block(T, xdeps, moddeps, hwar, phase="both", info=None):
                tsl = slice(T * TB, (T + 1) * TB)
                if phase == "apply":
                    i, rap, tr = info
                    return apply_(T, tsl, i, rap, tr, xdeps, moddeps, hwar)
                tok = None
                if phase == "sq":
                    out = []
                    for c in range(NCH):
                        i, ap, deps = sq.next()
                        ta = B.op("act", lambda e, ap=ap, c=c, tsl=tsl: e.activation(ap[:], xT[:, c, tsl], AF.Square),
                                  deps=deps + xdeps)
                        sq.wrote(i, ta)
                        out.append((i, ap, ta))
                    return out
                for c in range(NCH):
                    if phase == "st":
                        i, ap, ta = info[c]
                    else:
                        i, ap, deps = sq.next()
                        ta = B.op("act", lambda e, ap=ap, c=c, tsl=tsl: e.activation(ap[:], xT[:, c, tsl], AF.Square),
                                  deps=deps + xdeps)
                        sq.wrote(i, ta)
                    tok = mm_group(psum[:, statbank, :], [(ones_b[:], ap[:])], [ta, t_const], statbank,
                                   start=(c == 0), stop=(c == NCH - 1))
                    sq.read(i, tok)
                tl = B.op("act", lambda e: e.activation(lb[:], psum[:, statbank, :], AF.Ln, bias=EPS, scale=1.0 / D),
                          deps=[tok] + stt["lb_rd"] + xdeps)
                bank_rd[statbank].append(tl)
                i, rap, deps = rs.next()
                tr = B.op("act", lambda e, rap=rap: e.activation(rap[:], lb[:], AF.Exp, scale=-0.5), deps=deps + [tl])
                stt["lb_rd"] = [tr]
                rs.wrote(i, tr)
                if phase in ("stats", "st"):
                    return (i, rap, tr)
                return apply_(T, tsl, i, rap, tr, xdeps, moddeps, hwar)

            def apply_(T, tsl, i, rap, tr, xdeps, moddeps, hwar):
                th = None
                for c in range(NCH):
                    j, tap, deps = tmp.next()
                    td = B.op("dve", lambda e, tap=tap, rap=rap, c=c, tsl=tsl: e.scalar_tensor_tensor(
                        tap[:], xT[:, c, tsl], sc1p[:, c:c + 1], rap[:], ALU.mult, ALU.mult),
                        deps=deps + [tr] + xdeps + moddeps)
                    tmp.wrote(j, td)
                    rs.read(i, td)
                    th = B.op("act", lambda e, tap=tap, c=c, tsl=tsl: e.activation(hT[:, c, tsl], tap[:], AF.Identity, bias=sh[:, c:c + 1]),
                              deps=[td] + hwar + moddeps)
                    tmp.read(j, th)
                return th
            return block

        def ffn_phase(name, h_tok, hg, ada_iter, final_out, epilogue=None, early=None):
            gu_pe = {}
            sgb = Rot([palloc(F32, TB) for _ in range(2)])
            hid = Rot([palloc(BF16, 2 * TB).rearrange("p (j t) -> p j t", j=2) for _ in range(2)])
            gbanks, ubanks, dbanks = Rot([0, 1]), Rot([2, 3]), Rot([4, 5, 6, 7])
            x_last = {}
            out_toks = []
            wts = {}

            def get_w(g):
                if g not in wts:
                    slot, lt = w_get((name, g))
                    wts[g] = (slot[:, 0:2048].rearrange("p (k n) -> p k n", k=8),
                              slot[:, 2048:4096].rearrange("p (k n) -> p k n", k=8),
                              slot[:, 4096:6144].rearrange("p (j n) -> p j n", j=2), lt)
                return wts[g]

            def emit_gu(g, T):
                wg, wu, wd, lt = get_w(g)
                tsl = slice(T * TB, (T + 1) * TB)
                hi, hap, hdeps = hid.next()
                hw = []
                for j in range(2):
                    _, gb, _ = gbanks.next()
                    _, ub, _ = ubanks.next()
                    tg = mm_group(psum[:, gb, :], [(wg[:, kc, j * 128:(j + 1) * 128], hT[:, kc, tsl]) for kc in range(8)],
                                  [lt, h_tok[T]], gb)
                    tu = mm_group(psum[:, ub, :], [(wu[:, kc, j * 128:(j + 1) * 128], hT[:, kc, tsl]) for kc in range(8)],
                                  [lt, h_tok[T]], ub)
                    si, sap, sdeps = sgb.next()
                    ta = B.op("act", lambda e, sap=sap, gb=gb: e.activation(sap[:], psum[:, gb, :], AF.Silu), deps=sdeps + [tg])
                    bank_rd[gb].append(ta)
                    sgb.wrote(si, ta)
                    td = B.op("dve", lambda e, hap=hap, j=j, sap=sap, ub=ub: e.tensor_tensor(hap[:, j, :], sap[:], psum[:, ub, :], ALU.mult),
                              deps=hdeps + [ta, tu])
                    bank_rd[ub].append(td)
                    sgb.read(si, td)
                    hw.append(td)
                    gu_pe[(g, T)] = tu
                hid.wrote(hi, hw[-1])
                return (hi, hap, hw)

            def emit_down(g, T, hinfo):
                wg, wu, wd, lt = get_w(g)
                hi, hap, hw = hinfo
                tsl = slice(T * TB, (T + 1) * TB)
                last_pe = None
                for d in range(NCH):
                    _, db, _ = dbanks.next()
                    tdn = mm_group(psum[:, db, :], [(wd[:, j, d * 128:(d + 1) * 128], hap[:, j, :]) for j in range(2)], hw, db)
                    hid.read(hi, tdn)
                    last_pe = tdn
                    prev = x_last.get((d, T))
                    tx = B.op("dve", lambda e, d=d, db=db, tsl=tsl: e.scalar_tensor_tensor(
                        xT[:, d, tsl], psum[:, db, :], hg[:, d:d + 1], xT[:, d, tsl], ALU.mult, ALU.add),
                        deps=[tdn, prev] + hg_dep)
                    bank_rd[db].append(tx)
                    x_last[(d, T)] = tx
                    if final_out and g == NFG - 1:
                        out_toks.append(B.dma("sp", out_v[:, d, tsl], xT[:, d, tsl], s_out, deps=[tx]))
                return last_pe

            units = [(g, T) for g in range(NFG) for T in range(NT)]
            nxt = emit_gu(*units[0])
            for i, (g, T) in enumerate(units):
                cur = nxt
                defer = early is not None and i + 1 < len(units) and units[i + 1] == (1, 0)
                if i + 1 < len(units) and not defer:
                    nxt = emit_gu(*units[i + 1])
                if i == 0 and early is not None:
                    early()
                last_pe = emit_down(g, T, cur)
                if epilogue is not None and g == NFG - 1:
                    epilogue(T, [x_last[(NCH - 1, T)]], [gu_pe[(g, T)]])
                if T == NT - 1:
                    w_release((name, g), [last_pe])
                    if ada_iter is not None:
                        ada_iter(g)
                if defer:
                    nxt = emit_gu(*units[i + 1])
            xd = [x_last[(NCH - 1, T)] for T in range(NT)]
            return xd, out_toks

        s_out = B.new_sem("out")

        U_OFF = P_OFF + 8 * S * 2
        preset()
        nb0 = make_norm(modT[:, 0:8], der[:, 0:8], 6, P_OFF, nrs=4)
        st0 = [nb0(T, [x_tok[T], t_const], [], [], phase="stats") for T in range(NT)]
        tk = None
        for i in range(4):
            tk = ada_slice(i)
        mod_finish0(0, tk)
        ada_next = [6]

        def early0():
            tk = ada_slice(4)
            tk = ada_slice(5)
            mod_finish0(1, tk)

        def ada_iter(g):
            if g == 0:
                return
            if ada_next[0] < 18:
                i = ada_next[0]
                ada_next[0] += 1
                tk = ada_slice(i)
                if i == 11:
                    mod_finish(1, tk)
                if i == 17:
                    mod_finish(2, tk)
            if g == NFG - 1:
                while ada_next[0] < 18:
                    ada_iter(-1)

        po[0] = P_OFF + 14336 + 4096
        h_tok0 = [nb0(T, [x_tok[T], t_const], [mod_tok[0]], [], phase="apply", info=st0[T]) for T in range(NT)]
        nb1 = make_norm(modT[:, 24:32], der[:, 16:24], 6, U_OFF, nsq=8)
        h_tok1 = [None] * NT
        pend1 = []

        def flush1():
            while pend1:
                T, sqs, xtoks, hrd = pend1.pop(0)
                info = nb1(T, xtoks, [], [], phase="st", info=sqs)
                h_tok1[T] = nb1(T, xtoks, [mod_tok[1]], hrd, phase="apply", info=info)

        def epi1(T, xtoks, hrd):
            flush1()
            pend1.append((T, nb1(T, xtoks, [], [], phase="sq"), xtoks, hrd))

        xd, _ = ffn_phase("ffn1", h_tok0, der[:, 8:16], ada_iter, False, epilogue=(None if stage == "ffn1" else epi1), early=early0)
        flush1()

        def finish(xdeps):
            toks = []
            for T in range(NT):
                for d in range(NCH):
                    toks.append(B.dma("sp", out_v[:, d, T * TB:(T + 1) * TB], xT[:, d, T * TB:(T + 1) * TB], s_out, deps=xdeps))
            B.wait_only("sp", [(s_out, B.dma_cnt[s_out])])

        if stage == "ffn1":
            bt = B.barrier()
            finish(bt)
            B.emit()
            return nc

        class _Stop(Exception):
            pass

        def stop_if(name):
            if stage == name:
                raise _Stop()

        bt = B.barrier()
        for e_ in ("pe", "act", "dve"):
            B.wait_only(e_, [t_eb, t_cl])
        preset()
        dbg_dump = []
        g1 = modT[:, 40:48]
        bufA = palloc(BF16, 8 * S).rearrange("p (c t) -> p c t", c=8)
        assert po[0] == U_OFF
        h_tok = [None] * NT
        if stage == 'mnorm':
            finish(bt); B.emit(); return nc
        po[0] = U_OFF
        qz = palloc(BF16, 2 * S).rearrange("p (m t) -> p m t", m=2)
        kT = palloc(BF16, S)
        vh = palloc(BF16, S).rearrange("p (t e) -> p t e", e=128)
        Eb = Rot([palloc(BF16, 512).rearrange("p (m q) -> p m q", m=2) for _ in range(4)])
        rSb = Rot([palloc(F32, 512).rearrange("p (m q) -> p m q", m=2) for _ in range(1)])
        odb = Rot([palloc(F32, 256) for _ in range(2)])
        sqb = Rot([palloc(BF16, 256) for _ in range(2)])
        rdb = Rot([palloc(F32, 256) for _ in range(2)])
        psqb = Rot([palloc(BF16, TB) for _ in range(2)])
        plrb = Rot([palloc(F32, TB) for _ in range(2)])
        bo_b = palloc(BF16, 128)
        assert po[0] <= ARENA, po[0]

        Sb, Ob, Sumb, Pb, PbC = Rot([0, 1, 6]), Rot([2, 3]), Rot([4, 5]), Rot([7, 4, 5, 2, 3]), Rot([7])
        t_qz = B.op("dve", lambda e: e.memset(qz[:], 0.0), deps=bt)
        t_bob = B.op("dve", lambda e: e.tensor_copy(bo_b[:], bo_f[:]), deps=bt)
        q_rd, k_rd, v_rd = list(bt) + [t_qz], list(bt), list(bt)
        gq = cA[:, A_GQ:A_GQ + 1]
        gk = cA[:, A_GK:A_GK + 1]

        for h in range(H):
            slot, lt = w_get(("att", h))
            wq = slot[:, 0:1024].rearrange("p (k n) -> p k n", k=8)
            wk = slot[:, 1024:2048].rearrange("p (k n) -> p k n", k=8)
            wv = slot[:, 2048:3072].rearrange("p (k n) -> p k n", k=8)
            units = [("q", T) for T in range(NT)] + [("k", T) for T in range(NT)]
            q_tok, k_tok = [None] * NT, [None] * NT
            pend = None

            def proj_mm(u):
                kind, T = u
                tsl = slice(T * TB, (T + 1) * TB)
                w = wq if kind == "q" else wk
                _, pb, _ = Pb.next()
                tp = mm_group(psum[:, pb, :], [(w[:, kc, :], hT[:, kc, tsl]) for kc in range(8)], [lt], pb)
                si, sq_ap, sdeps = psqb.next()
                ts = B.op("act", lambda e, pb=pb, sq_ap=sq_ap: e.activation(sq_ap[:], psum[:, pb, :], AF.Square), deps=[tp] + sdeps)
                psqb.wrote(si, ts)
                bank_rd[pb].append(ts)
                return (kind, T, tsl, pb, tp, si, sq_ap, ts)

            def proj_fin(st_):
                kind, T, tsl, pb, tp, si, sq_ap, ts = st_
                _, sbk, _ = Sb.next()
                tst = mm_group(psum[:, sbk, :], [(bo_b[:], sq_ap[:])], [ts, t_bob], sbk)
                psqb.read(si, tst)
                li, lr, ldeps = plrb.next()
                tl = B.op("act", lambda e, sbk=sbk, lr=lr: e.activation(lr[:], psum[:, sbk, :], AF.Ln, bias=EPS, scale=1.0 / 64),
                          deps=[tst] + ldeps)
                bank_rd[sbk].append(tl)
                tr = B.op("act", lambda e, lr=lr: e.activation(lr[:], lr[:], AF.Exp, scale=-0.5), deps=[tl])
                if kind == "q":
                    B.op("dve", lambda e, pb=pb, tsl=tsl, lr=lr: e.scalar_tensor_tensor(
                        qz[0:64, 0, tsl], psum[0:64, pb, :], gq[0:64, :], lr[0:64, :], ALU.mult, ALU.mult),
                        deps=[tr, tp, t_cl] + q_rd)
                    tq = B.op("dve", lambda e, pb=pb, tsl=tsl, lr=lr: e.scalar_tensor_tensor(
                        qz[64:128, 1, tsl], psum[64:128, pb, :], gq[64:128, :], lr[64:128, :], ALU.mult, ALU.mult),
                        deps=[tr, tp, t_cl] + q_rd)
                    q_tok[T] = tq
                else:
                    tq = B.op("dve", lambda e, pb=pb, tsl=tsl, lr=lr: e.scalar_tensor_tensor(
                        kT[:, tsl], psum[:, pb, :], gk, lr[:], ALU.mult, ALU.mult), deps=[tr, tp, t_cl] + k_rd)
                    k_tok[T] = tq
                bank_rd[pb].append(tq)
                plrb.wrote(li, tq)

            for u in units:
                st_ = proj_mm(u)
                if pend is not None:
                    proj_fin(pend)
                pend = st_
            v_tok = []
            last_v = None
            for tg in range(4):
                _, pb, _ = Pb.next()
                tv = None
                for tt in range(4):
                    t0 = (tg * 4 + tt) * 128
                    tv = mm_group(psum[:, pb, tt * 128:(tt + 1) * 128],
                                  [(hT[:, kc, t0:t0 + 128], wv[:, kc, :]) for kc in range(8)],
                                  [lt], pb, start=True)
                if pend is not None:
                    proj_fin(pend)
                    pend = None
                tc = B.op("dve", lambda e, pb=pb, tg=tg: e.tensor_copy(
                    vh[:, tg * 4:(tg + 1) * 4, :], psum[:, pb, :].rearrange("p (t e) -> p t e", e=128)),
                    deps=[tv] + v_rd)
                bank_rd[pb].append(tc)
                v_tok.append(tc)
                last_v = tv
            w_release(("att", h), [last_v])
            q_rd, k_rd, v_rd = [], [], []
            if stage == 'proj0':
                bt = B.barrier(); finish(bt); B.emit(); return nc

            pairs = [(Qb, j) for Qb in range(8) for j in range(2 * Qb + 2)]
            blk = {}

            def emit_S(pair):
                Qb, j = pair
                q0b = Qb * 256
                r = j - 2 * Qb
                qs = 128 if r == 1 else 0
                _, sbk, _ = Sb.next()
                Sv = psum[:, sbk, :].rearrange("p (m q) -> p m q", m=2)
                if qs == 0:
                    ts = B.op("pe", lambda e, Sv=Sv, j=j, q0b=q0b: e.matmul(
                        Sv[:, :, :], kT[:, j * 128:(j + 1) * 128], qz[:, :, q0b:q0b + 256], start=True, stop=True),
                        deps=[q_tok[q0b // TB], k_tok[(j * 128) // TB]] + bank_rd[sbk], signal=True)
                else:
                    for m in range(2):
                        ts = B.op("pe", lambda e, Sv=Sv, j=j, q0b=q0b, m=m: e.matmul(
                            Sv[:, m, 128:256], kT[:, j * 128:(j + 1) * 128], qz[:, m, q0b + 128:q0b + 256], start=True, stop=True),
                            deps=([q_tok[q0b // TB], k_tok[(j * 128) // TB]] + bank_rd[sbk]) if m == 0 else [], signal=(m == 1))
                bank_rd[sbk] = []
                ei, E, edeps = Eb.next()
                te = B.op("act", lambda e, E=E, Sv=Sv, qs=qs: e.activation(E[:, :, qs:256], Sv[:, :, qs:256], AF.Exp, scale=0.125),
                          deps=edeps + [ts])
                bank_rd[sbk].append(te)
                fix = []
                if r == -1:
                    fix = [(0, EBs)]
                elif r == 0:
                    fix = [(0, EBd), (128, EBs)]
                elif r == 1:
                    fix = [(128, EBd)]
                tl = te
                for (c0, EBt) in fix:
                    a = EBt[:, h, :]
                    ebb = bass.AP(a.tensor, a.offset, [list(a.ap[0]), [0, 2], list(a.ap[1])])
                    tl = B.op("dve", lambda e, E=E, c0=c0, ebb=ebb: e.tensor_tensor(
                        E[:, :, c0:c0 + 128], E[:, :, c0:c0 + 128], ebb, ALU.mult), deps=[te, t_eb])
                Eb.wrote(ei, tl)
                return (ei, E, qs, tl, te)

            def emit_PV(pair, einfo):
                Qb, j = pair
                ei, E, qs, tl, te = einfo
                nj = 2 * Qb + 2
                if j == 0:
                    _, ob, _ = Ob.next()
                    _, smb, _ = Sumb.next()
                    blk[Qb] = (ob, smb)
                ob, smb = blk[Qb]
                Ov = psum[:, ob, :].rearrange("p (m q) -> p m q", m=2)
                Smv = psum[:, smb, :].rearrange("p (m q) -> p m q", m=2)
                first, lastj = (j == 0), (j == nj - 1)
                if qs == 0:
                    B.op("pe", lambda e, Ov=Ov, E=E, j=j, first=first, lastj=lastj: e.matmul(
                        Ov[:, :, :], vh[:, j, :], E[:, :, :], start=first, stop=lastj),
                        deps=[tl, te, v_tok[j // 4]] + (bank_rd[ob] if first else []), signal=False)
                else:
                    for m in range(2):
                        B.op("pe", lambda e, Ov=Ov, E=E, j=j, m=m, first=first, lastj=lastj: e.matmul(
                            Ov[:, m, 128:256], vh[:, j, :], E[:, m, 128:256], start=first, stop=(lastj and m == 1)),
                            deps=([tl, te, v_tok[j // 4]] + (bank_rd[ob] if first else [])) if m == 0 else [], signal=False)
                if first:
                    bank_rd[ob] = []
                if qs == 0:
                    tpv = B.op("pe", lambda e, Smv=Smv, E=E, first=first, lastj=lastj: e.matmul(
                        Smv[:, :, :], ones_b[:], E[:, :, :], start=first, stop=lastj),
                        deps=(bank_rd[smb] if first else []), signal=True)
                else:
                    for m in range(2):
                        tpv = B.op("pe", lambda e, Smv=Smv, E=E, m=m, first=first, lastj=lastj: e.matmul(
                            Smv[:, m, 128:256], ones_b[:], E[:, m, 128:256], start=first, stop=(lastj and m == 1)),
                            deps=(bank_rd[smb] if (first and m == 0) else []), signal=(m == 1))
                if first:
                    bank_rd[smb] = []
                Eb.read(ei, tpv)
                return tpv

            def fin_A(Qb, tpv, stt):
                ob, smb = blk[Qb]
                Ov = psum[:, ob, :].rearrange("p (m q) -> p m q", m=2)
                Smv = psum[:, smb, :].rearrange("p (m q) -> p m q", m=2)
                ri, rS, rdeps = rSb.next()
                t1 = B.op("act", lambda e, rS=rS, Smv=Smv: e.activation(rS[:], Smv[:], AF.Ln), deps=rdeps + [tpv])
                bank_rd[smb].append(t1)
                t2 = B.op("act", lambda e, rS=rS: e.activation(rS[:], rS[:], AF.Exp, scale=-1.0), deps=[t1])
                t3 = B.op("dve", lambda e, rS=rS, Ov=Ov: e.tensor_tensor(rS[:], Ov[:], rS[:], ALU.mult), deps=[t2, tpv])
                bank_rd[ob].append(t3)
                oi, od, odeps = odb.next()
                t4 = B.op("dve", lambda e, od=od, rS=rS: e.scalar_tensor_tensor(od[:], rS[:, 1, :], neg_lam, rS[:, 0, :], ALU.mult, ALU.add),
                          deps=odeps + [t3, t_lam])
                rSb.wrote(ri, t4)
                stt.update(oi=oi, od=od, t4=t4)

            def fin_B(Qb, stt):
                od, t4 = stt["od"], stt["t4"]
                qi, sqv, qdeps = sqb.next()
                t5 = B.op("dve", lambda e, sqv=sqv, od=od: e.tensor_tensor(sqv[:], od[:], od[:], ALU.mult), deps=qdeps + [t4])
                _, pb, _ = PbC.next()
                t6 = mm_group(psum[:, pb, 0:256], [(ones_b[:], sqv[:])], [t5], pb)
                sqb.read(qi, t6)
                stt.update(pb=pb, t6=t6)

            def fin_C(Qb, stt):
                od, t4, pb, t6, oi = stt["od"], stt["t4"], stt["pb"], stt["t6"], stt["oi"]
                q0b = Qb * 256
                di, rd, ddeps = rdb.next()
                t7 = B.op("act", lambda e, rd=rd, pb=pb: e.activation(rd[:], psum[:, pb, 0:256], AF.Ln, bias=EPS, scale=1.0 / 128),
                          deps=ddeps + [t6])
                bank_rd[pb].append(t7)
                t8 = B.op("act", lambda e, rd=rd: e.activation(rd[:], rd[:], AF.Exp, scale=-0.5), deps=[t7])
                t9 = B.op("dve", lambda e, od=od, rd=rd, h=h, q0b=q0b: e.scalar_tensor_tensor(
                    bufA[:, h, q0b:q0b + 256], od[:], gsub8, rd[:], ALU.mult, ALU.mult), deps=[t8, t4, t_lam])
                odb.read(oi, t9)
                odb.wrote(oi, t4)
                rdb.wrote(di, t8)
                rdb.read(di, t9)

            sched = {}

            def at(i, fn):
                sched.setdefault(i, []).append(fn)

            PF = 2
            sq_ = [emit_S(pairs[k]) for k in range(PF)]
            last_tpv = None
            npairs = len(pairs)
            for i, pair in enumerate(pairs):
                if i + PF < npairs:
                    sq_.append(emit_S(pairs[i + PF]))
                cur = sq_.pop(0)
                tpv = emit_PV(pair, cur)
                last_tpv = tpv
                Qb, j = pair
                if j == 2 * Qb + 1:
                    stt = {}
                    e_next = i + 2 * (Qb + 1) + 2 if Qb < 7 else 10 ** 9
                    at(i + 1, lambda Qb=Qb, tpv=tpv, stt=stt: fin_A(Qb, tpv, stt))
                    at(min(i + 12, e_next), lambda Qb=Qb, stt=stt: fin_B(Qb, stt))
                    at(min(i + 16, e_next + 1), lambda Qb=Qb, stt=stt: fin_C(Qb, stt))
                for fn in sched.pop(i, []):
                    fn()
            for i in sorted(sched):
                for fn in sched[i]:
                    fn()
            q_rd.append(last_tpv)
            k_rd.append(last_tpv)
            v_rd.append(last_tpv)
            if stage == 'head0':
                bt = B.barrier(); finish(bt); B.emit(); return nc

        def branch_out(pname, wname, src, mbuf, sgbuf, bt, epilogue=None):
            rots = (Rot([0, 1]), Rot([2, 3]), Rot([4, 5, 6, 7]))
            m_tok = {}
            sg_rd = list(bt)
            for i in range(4):
                slot, lt = w_get((pname, i))
                wp = slot[:, 0:2048].rearrange("p (k n) -> p k n", k=8)
                wgt = slot[:, 2048:4096].rearrange("p (k n) -> p k n", k=8)
                last = None
                for T in range(NT):
                    tsl = slice(T * TB, (T + 1) * TB)
                    for dd in range(2):
                        d = 2 * i + dd
                        _, yb, _ = rots[0].next()
                        _, gb, _ = rots[1].next()
                        ty = mm_group(psum[:, yb, :], [(wp[:, kc, dd * 128:(dd + 1) * 128], src[:, kc, tsl]) for kc in range(8)],
                                      [lt] + bt, yb)
                        tg = mm_group(psum[:, gb, :], [(wgt[:, kc, dd * 128:(dd + 1) * 128], hT[:, kc, tsl]) for kc in range(8)],
                                      [lt], gb)
                        last = tg
                        ta = B.op("act", lambda e, gb=gb: e.activation(sgbuf[:], psum[:, gb, :], AF.Sigmoid), deps=[tg] + sg_rd)
                        bank_rd[gb].append(ta)
                        tm = B.op("dve", lambda e, d=d, yb=yb, tsl=tsl: e.tensor_tensor(mbuf[:, d, tsl], sgbuf[:], psum[:, yb, :], ALU.mult),
                                  deps=[ta, ty] + bt)
                        bank_rd[yb].append(tm)
                        sg_rd = [tm]
                        m_tok[(d, T)] = tm
                w_release((pname, i), [last])
            last_proj = last
            for i in range(2):
                slot, lt = w_get((wname, i))
                wo = slot[:, 0:4096].rearrange("p (k n) -> p k n", k=8)
                last = None
                for T in range(NT):
                    tsl = slice(T * TB, (T + 1) * TB)
                    for dd in range(4):
                        d = 4 * i + dd
                        _, ob, _ = rots[2].next()
                        to = mm_group(psum[:, ob, :], [(wo[:, kc, dd * 128:(dd + 1) * 128], mbuf[:, kc, tsl]) for kc in range(8)],
                                      [lt, m_tok[(7, T)]], ob)
                        last = to
                        tx = B.op("dve", lambda e, d=d, ob=ob, tsl=tsl: e.scalar_tensor_tensor(
                            xT[:, d, tsl], psum[:, ob, :], g1[:, d:d + 1], xT[:, d, tsl], ALU.mult, ALU.add),
                            deps=[to, mod_tok[1]] + bt)
                        bank_rd[ob].append(tx)
                    if epilogue is not None and i == 1:
                        epilogue(T, [tx], [last_proj])
                w_release((wname, i), [last])

        bt = B.barrier()
        po[0] = U_OFF
        mbuf = palloc(BF16, 8 * S).rearrange("p (c t) -> p c t", c=8)
        sgbuf = palloc(F32, TB)
        assert po[0] <= ARENA, po[0]
        branch_out("aproj", "woa", bufA, mbuf, sgbuf, bt)

        if stage == "attn":
            bt = B.barrier()
            finish(bt)
            B.emit()
            return nc

        bt = B.barrier()
        po[0] = U_OFF
        cBs = palloc(F32, NB)
        wTm = palloc(BF16, 1024).rearrange("p (g t) -> p g t", g=8)
        gfb = Rot([palloc(F32, 1024) for _ in range(2)])
        vnb = Rot([palloc(BF16, 1024) for _ in range(2)])
        tmpb = Rot([palloc(F32, TB) for _ in range(2)])
        stbb = Rot([palloc(F32, 16) for _ in range(2)])
        assert po[0] <= ARENA, po[0]
        gu = bufA
        s_cb = B.new_sem("cB")
        t_cB = B.dma("sp", cBs[:], cB_d, s_cb, deps=bt)
        tri = cBs[:, B_TRI:B_TRI + 128]
        wst = cBs[:, B_WST:B_WST + 1024].rearrange("p (g t) -> p g t", g=8)
        lng_c = cBs[:, B_LNG:B_LNG + 8]
        lnb_c = cBs[:, B_LNB:B_LNB + 8]
        bs_bc = cBs[:, B_BS:B_BS + 1024]
        t_w = None
        for g in range(8):
            t_w = B.op("dve", lambda e, g=g: e.tensor_tensor(wTm[:, g, :], wst[:, g, :], tri, ALU.mult), deps=[t_cB] + bt)
        t_bg = None
        for half in range(2):
            trs = mm_group(psum[:, 6 + half, :], [(ones_b[:], wTm[:, 4 * half:4 * half + 4, :])], [t_w] + bt, 6 + half)
            for gg in range(4):
                g = 4 * half + gg
                t_bg = B.op("dve", lambda e, g=g, gg=gg, half=half: e.scalar_tensor_tensor(
                    bs_bc[:, g * 128:(g + 1) * 128], psum[:, 6 + half, gg * 128:(gg + 1) * 128], lnb_c[:, g:g + 1],
                    bs_bc[:, g * 128:(g + 1) * 128], ALU.mult, ALU.add), deps=[trs, t_cB])
            bank_rd[6 + half].append(t_bg)
        u_tok = {}
        Ub = Rot([0, 1, 2, 3])
        for i in range(2):
            slot, lt = w_get(("u", i))
            w = slot[:, 0:4096].rearrange("p (k n) -> p k n", k=8)
            last = None
            for T in range(NT):
                tsl = slice(T * TB, (T + 1) * TB)
                for cc in range(4):
                    c = 4 * i + cc
                    _, ub, _ = Ub.next()
                    tu = mm_group(psum[:, ub, :], [(w[:, kc, cc * 128:(cc + 1) * 128], hT[:, kc, tsl]) for kc in range(8)],
                                  [lt] + bt, ub)
                    last = tu
                    ta = B.op("act", lambda e, c=c, ub=ub, tsl=tsl: e.activation(gu[:, c, tsl], psum[:, ub, :], AF.Gelu), deps=[tu] + bt)
                    bank_rd[ub].append(ta)
                    u_tok[(c, T)] = ta
            w_release(("u", i), [last])
        slot0, lt0 = w_get(("gv", 0))
        slot1, lt1 = w_get(("gv", 1))
        wgv = [slot0[:, 0:4096].rearrange("p (k n) -> p k n", k=8), slot1[:, 0:4096].rearrange("p (k n) -> p k n", k=8)]
        Gb, Fb = Rot([0, 1, 2, 3]), Rot([4, 5, 6, 7])
        last_gv = None

        def gv_A(tt):
            nonlocal last_gv
            t0 = tt * 128
            gi, gfa, gdeps = gfb.next()
            si, sta, sdeps = stbb.next()
            tstat = []
            for half in range(2):
                _, gb, _ = Gb.next()
                tg = mm_group(psum[:, gb, :], [(hT[:, kc, t0:t0 + 128], wgv[half][:, kc, :]) for kc in range(8)], [lt0, lt1] + bt, gb)
                last_gv = tg
                ta = B.op("act", lambda e, half=half, gb=gb, gfa=gfa: e.activation(gfa[:, half * 512:(half + 1) * 512], psum[:, gb, :], AF.Gelu),
                          deps=[tg] + gdeps)
                bank_rd[gb].append(ta)
                tstat.append(B.op("dve", lambda e, half=half, gfa=gfa, sta=sta: e.bn_stats(sta[:, half * 6:(half + 1) * 6], gfa[:, half * 512:(half + 1) * 512]),
                                  deps=[ta] + sdeps))
            t = B.op("dve", lambda e, sta=sta: e.bn_aggr(sta[:, 12:14], sta[:, 0:12]), deps=tstat)
            t = B.op("act", lambda e, sta=sta: e.activation(sta[:, 14:15], sta[:, 13:14], AF.Ln, bias=EPS), deps=[t])
            t = B.op("act", lambda e, sta=sta: e.activation(sta[:, 14:15], sta[:, 14:15], AF.Exp, scale=-0.5), deps=[t])
            t = B.op("dve", lambda e, sta=sta: e.scalar_tensor_tensor(sta[:, 15:16], sta[:, 12:13], -1.0, sta[:, 14:15], ALU.mult, ALU.mult), deps=[t])
            vi, vn, vdeps = vnb.next()
            tvn = B.op("act", lambda e, gfa=gfa, sta=sta, vn=vn: e.activation(vn[:], gfa[:], AF.Identity, bias=sta[:, 15:16], scale=sta[:, 14:15]),
                       deps=[t] + vdeps)
            stbb.wrote(si, tvn)
            gfb.wrote(gi, tvn)
            vnb.wrote(vi, tvn)
            return (vi, vn, tvn)

        def gv_B(tt, info):
            vi, vn, tvn = info
            t0 = tt * 128
            for fi in range(2):
                _, fb, _ = Fb.next()
                tf = None
                for gg in range(4):
                    g = fi * 4 + gg
                    tf = mm_group(psum[:, fb, gg * 128:(gg + 1) * 128], [(vn[:, g * 128:(g + 1) * 128], wTm[:, g, :])],
                                  [tvn, t_w], fb, start=True)
                vnb.read(vi, tf)
                pi, tp, pdeps = tmpb.next()
                t1 = None
                for gg in range(4):
                    g = fi * 4 + gg
                    t1 = B.op("dve", lambda e, tp=tp, fb=fb, g=g, gg=gg: e.scalar_tensor_tensor(
                        tp[:, gg * 128:(gg + 1) * 128], psum[:, fb, gg * 128:(gg + 1) * 128], lng_c[:, g:g + 1],
                        bs_bc[:, g * 128:(g + 1) * 128], ALU.mult, ALU.add), deps=[tf, t_cB, t_bg] + pdeps)
                bank_rd[fb].append(t1)
                t2 = B.op("dve", lambda e, tp=tp, fi=fi, t0=t0: e.tensor_tensor(
                    gu[:, 4 * fi:4 * fi + 4, t0:t0 + 128], gu[:, 4 * fi:4 * fi + 4, t0:t0 + 128],
                    tp[:].rearrange("p (g t) -> p g t", g=4), ALU.mult),
                    deps=[t1] + [u_tok[(4 * fi + gg, tt // 4)] for gg in range(4)])
                tmpb.wrote(pi, t2)

        nxt = gv_A(0)
        for tt in range(16):
            cur = nxt
            if tt + 1 < 16:
                nxt = gv_A(tt + 1)
            gv_B(tt, cur)
        w_release(("gv", 0), [last_gv])
        w_release(("gv", 1), [last_gv])

        bt = B.barrier()
        po[0] = U_OFF
        mbuf = palloc(BF16, 8 * S).rearrange("p (c t) -> p c t", c=8)
        sgbuf = palloc(F32, TB)
        nb2 = make_norm(modT[:, 48:56], der[:, 24:32], 7, P_OFF, nsq=8)
        h_tok2 = [None] * NT
        pend2 = []

        def flush2():
            while pend2:
                T, sqs, xtoks, hrd = pend2.pop(0)
                info = nb2(T, xtoks + hrd, [], [], phase="st", info=sqs)
                h_tok2[T] = nb2(T, xtoks + hrd, [mod_tok[2]], hrd, phase="apply", info=info)

        def epi2(T, xtoks, hrd):
            flush2()
            pend2.append((T, nb2(T, xtoks + hrd, [], [], phase="sq"), xtoks, hrd))

        branch_out("bproj", "wob", gu, mbuf, sgbuf, bt, epilogue=(None if stage == "mixer" else epi2))
        flush2()

        if stage == "mixer":
            bt = B.barrier()
            finish(bt)
            B.emit()
            return nc

        bt = B.barrier()
        po[0] = P_OFF + 20480
        _, out_toks = ffn_phase("ffn2", h_tok2, der[:, 32:40], None, True)
        B.wait_only("sp", [(s_out, B.dma_cnt[s_out])])
        B.emit()
    return nc


_CACHE = {}


def kernel(**inp):
    x = np.asarray(inp["x"], np.float32)
    c = np.asarray(inp["c"], np.float32)
    inp = {k: np.asarray(v) for k, v in inp.items()}
    wblob = build_wblob(inp)
    cA, cB, cC = build_consts(inp)
    if "nc" not in _CACHE:
        _CACHE["nc"] = build_program("full")
    nc = _CACHE["nc"]
    in_maps = []
    for b in range(8):
        in_maps.append({"xT": np.ascontiguousarray(x[b].T), "cv": np.ascontiguousarray(c[b].reshape(8, 128).T),
                        "cA": cA, "cB": cB, "cC": cC, "wblob": wblob})
    res = run_bass_kernel_spmd(nc, in_maps, core_ids=list(range(8)))
    out = np.stack([np.asarray(res.results[b]["outT"]).T for b in range(8)])
    return np.ascontiguousarray(out.astype(np.float32))
```
